# Optimizing a Trainium2 kernel written in Bass

```python
import jax, jax.numpy as jnp
from jax import lax
import numpy as np

D_MODEL = 1024
BATCH = 32
SEQ = 2048
DEPTH = 4

MEM_LEN = 256
POOL_WIDTH = D_MODEL // 4
POOL_GROUPS = 4
POOL_WINDOWS = (2, 4, 8, 16)
HEAD_DIM = 64
ATTN_WIDTH = 3 * D_MODEL // 8
ATTN_HEADS = ATTN_WIDTH // HEAD_DIM
DILATED_PATTERNS = ((128, 1), (512, 4), (2048, 16))
ATTN_BLOCK = 128
LRU_WIDTH = 3 * D_MODEL // 8
LRU_BLOCKS = 6
LRU_CONV = 4
LRU_C = 8.0
MIX_WIDTH = POOL_WIDTH + ATTN_WIDTH + LRU_WIDTH
IN_WIDTH = POOL_WIDTH + 3 * ATTN_WIDTH + 2 * LRU_WIDTH
MEM_HEADS = 4
MEM_HEAD_DIM = D_MODEL // MEM_HEADS
D_FF = 2816
FFN_CONV = 3
EPS = 1e-6

kernel_name = 'hybrid_pool_dilattn_rglru_convffn'


def rmsnorm(x, g):
    xf = x.astype(jnp.float32)
    xf = xf * lax.rsqrt(jnp.mean(xf * xf, axis=-1, keepdims=True) + EPS)
    return (xf * g.astype(jnp.float32)).astype(x.dtype)


def causal_dwconv(x, w, b):
    K = w.shape[0]
    S = x.shape[1]
    xp = jnp.pad(x, ((0, 0), (K - 1, 0), (0, 0)))
    out = b
    for k in range(K):
        out = out + w[k] * xp[:, K - 1 - k:K - 1 - k + S]
    return out


def alibi_slopes(n):
    return jnp.asarray([2.0 ** (-8.0 * (h + 1) / n) for h in range(n)], dtype=jnp.float32)


def pool_mixer(u, pool_w, pool_scale):
    B, S, _ = u.shape
    gw = POOL_WIDTH // POOL_GROUPS
    uf = u.astype(jnp.float32)
    csum = jnp.cumsum(uf, axis=1)
    cp = jnp.concatenate([jnp.zeros_like(csum[:, :1]), csum], axis=1)
    outs = []
    for g, w in enumerate(POOL_WINDOWS):
        sl = slice(g * gw, (g + 1) * gw)
        end = cp[:, 1:, sl]
        start = jnp.concatenate([jnp.zeros((B, w - 1, gw), jnp.float32), cp[:, :S - w + 1, sl]], axis=1)
        count = jnp.minimum(jnp.arange(1, S + 1), w).astype(jnp.float32)[None, :, None]
        outs.append((end - start) / count - uf[:, :, sl])
    pooled = jnp.stack(outs, axis=2)
    mixed = jnp.einsum('bsgc,gcd->bsgd', pooled, pool_w.astype(jnp.float32)).reshape(B, S, POOL_WIDTH)
    return (mixed * pool_scale.astype(jnp.float32)).astype(u.dtype)


def sliding_window_attn(q, k, v, slopes_step, window):
    N, L, H, hd = q.shape
    Q = ATTN_BLOCK
    nb = -(-L // Q)
    Lp = nb * Q
    pad = ((0, 0), (0, Lp - L), (0, 0), (0, 0))
    q, k, v = jnp.pad(q, pad), jnp.pad(k, pad), jnp.pad(v, pad)
    qb = q.reshape(N, nb, Q, H, hd)

    def with_prev(t):
        cur = t.reshape(N, nb, Q, H, hd)
        prev = jnp.concatenate([jnp.zeros_like(cur[:, :1]), cur[:, :-1]], axis=1)
        return jnp.concatenate([prev, cur], axis=2)

    kb, vb = with_prev(k), with_prev(v)
    s = jnp.einsum('nbqhd,nbkhd->nbhqk', qb, kb) * (hd ** -0.5)
    dist = Q + jnp.arange(Q)[:, None] - jnp.arange(2 * Q)[None, :]
    key_pos = jnp.arange(nb)[:, None] * Q - Q + jnp.arange(2 * Q)[None, :]
    valid = ((dist >= 0) & (dist <= window))[None, :, :] & (key_pos >= 0)[:, None, :]
    s = s - slopes_step[:, None, None] * dist.astype(jnp.float32)
    s = jnp.where(valid[None, :, None], s, -jnp.inf)
    m = jnp.max(s, axis=-1, keepdims=True)
    p = jnp.exp(s - m)
    den = jnp.sum(p, axis=-1, keepdims=True)
    o = jnp.einsum('nbhqk,nbkhd->nbqhd', p, vb) / jnp.transpose(den, (0, 1, 3, 2, 4))
    lse = jnp.transpose((m + jnp.log(den))[..., 0], (0, 1, 3, 2))
    return o.reshape(N, Lp, H, hd)[:, :L], lse.reshape(N, Lp, H)[:, :L]


def dilated_attention(q, k, v, slopes):
    B, S, H, hd = q.shape
    outs, lses = [], []
    for window, d in DILATED_PATTERNS:
        L = S // d

        def to_sub(t):
            return t.reshape(B, L, d, H, hd).transpose(0, 2, 1, 3, 4).reshape(B * d, L, H, hd)

        o, lse = sliding_window_attn(to_sub(q), to_sub(k), to_sub(v), slopes * d, window // d)
        outs.append(o.reshape(B, d, L, H, hd).transpose(0, 2, 1, 3, 4).reshape(B, S, H, hd))
        lses.append(lse.reshape(B, d, L, H).transpose(0, 2, 1, 3).reshape(B, S, H))
    wts = jax.nn.softmax(jnp.stack(lses, axis=0), axis=0)
    return jnp.sum(wts[..., None] * jnp.stack(outs, axis=0), axis=0)


def rg_lru(x, w_a, b_a, w_x, b_x, lam):
    B, S, C = x.shape
    xb = x.reshape(B, S, LRU_BLOCKS, C // LRU_BLOCKS)
    r = jax.nn.sigmoid(jnp.einsum('bsgc,gcd->bsgd', xb, w_a.astype(jnp.float32)).reshape(B, S, C) + b_a.astype(jnp.float32))
    i = jax.nn.sigmoid(jnp.einsum('bsgc,gcd->bsgd', xb, w_x.astype(jnp.float32)).reshape(B, S, C) + b_x.astype(jnp.float32))
    log_a = -LRU_C * r * jax.nn.softplus(-lam.astype(jnp.float32))
    a = jnp.exp(log_a)
    bterm = jnp.sqrt(-jnp.expm1(2.0 * log_a)) * (i * x)

    def combine(c1, c2):
        a1, b1 = c1
        a2, b2 = c2
        return a1 * a2, a2 * b1 + b2

    _, h = lax.associative_scan(combine, (a, bterm), axis=1)
    return h


def mixer_block(h, w_in, pool_w, pool_scale, q_gain, k_gain, lru_conv_w, lru_conv_b,
                lru_wa, lru_ba, lru_wx, lru_bx, lru_lambda, w_out, slopes):
    B, S, _ = h.shape
    P, A, R = POOL_WIDTH, ATTN_WIDTH, LRU_WIDTH
    proj = jnp.einsum('bsd,de->bse', h, w_in)
    u_pool, q, k, v, x_lru, y_lru = jnp.split(proj, [P, P + A, P + 2 * A, P + 3 * A, P + 3 * A + R], axis=-1)
    out_pool = pool_mixer(u_pool, pool_w, pool_scale)
    qh = rmsnorm(q.reshape(B, S, ATTN_HEADS, HEAD_DIM), q_gain).astype(jnp.float32)
    kh = rmsnorm(k.reshape(B, S, ATTN_HEADS, HEAD_DIM), k_gain).astype(jnp.float32)
    vh = v.reshape(B, S, ATTN_HEADS, HEAD_DIM).astype(jnp.float32)
    out_attn = dilated_attention(qh, kh, vh, slopes).reshape(B, S, A).astype(h.dtype)
    xc = causal_dwconv(x_lru, lru_conv_w, lru_conv_b).astype(jnp.float32)
    hr = rg_lru(xc, lru_wa, lru_ba, lru_wx, lru_bx, lru_lambda)
    out_lru = (hr * jax.nn.gelu(y_lru.astype(jnp.float32))).astype(h.dtype)
    mixed = jnp.concatenate([out_pool, out_attn, out_lru], axis=-1)
    return jnp.einsum('bse,ed->bsd', mixed, w_out)


def memory_attn(h, mem_n, w_q, w_kv, qg, kg, w_o):
    B, S, _ = h.shape
    M = mem_n.shape[1]
    q = rmsnorm(jnp.einsum('bsd,de->bse', h, w_q).reshape(B, S, MEM_HEADS, MEM_HEAD_DIM), qg)
    k, v = jnp.split(jnp.einsum('bmd,de->bme', mem_n, w_kv), 2, axis=-1)
    k = rmsnorm(k.reshape(B, M, MEM_HEADS, MEM_HEAD_DIM), kg)
    v = v.reshape(B, M, MEM_HEADS, MEM_HEAD_DIM)
    s = jnp.einsum('bshd,bmhd->bhsm', q.astype(jnp.float32), k.astype(jnp.float32)) * (MEM_HEAD_DIM ** -0.5)
    p = jax.nn.softmax(s, axis=-1)
    o = jnp.einsum('bhsm,bmhd->bshd', p, v.astype(jnp.float32)).reshape(B, S, D_MODEL).astype(h.dtype)
    return jnp.einsum('bse,ed->bsd', o, w_o)


def conv_ffn(h, w_up, conv_w, conv_b, w_down):
    g, u = jnp.split(jnp.einsum('bsd,df->bsf', h, w_up), 2, axis=-1)
    g = causal_dwconv(g, conv_w, conv_b)
    return jnp.einsum('bsf,fd->bsd', jax.nn.gelu(g) * u, w_down)


def setup_inputs(seed: int = 0) -> dict:
    key = jax.random.key(seed)
    ks = jax.random.split(key, 32)
    L = DEPTH
    gw = POOL_WIDTH // POOL_GROUPS
    bw = LRU_WIDTH // LRU_BLOCKS

    def nrm(k, shape, scale):
        return jax.random.normal(k, shape, jnp.float32) * scale

    a0 = jax.random.uniform(ks[14], (L, LRU_WIDTH), jnp.float32, 0.9, 0.999)
    return {
        'x': nrm(ks[0], (BATCH, SEQ, D_MODEL), 1.0),
        'mem': nrm(ks[1], (BATCH, MEM_LEN, D_MODEL), 1.0),
        'norm_mix': 1.0 + nrm(ks[2], (L, D_MODEL), 0.1),
        'w_in': nrm(ks[3], (L, D_MODEL, IN_WIDTH), D_MODEL ** -0.5),
        'pool_w': nrm(ks[4], (L, POOL_GROUPS, gw, gw), gw ** -0.5),
        'pool_scale': 1.0 + nrm(ks[5], (L, POOL_WIDTH), 0.1),
        'q_gain': 1.0 + nrm(ks[6], (L, HEAD_DIM), 0.1),
        'k_gain': 1.0 + nrm(ks[7], (L, HEAD_DIM), 0.1),
        'lru_conv_w': nrm(ks[8], (L, LRU_CONV, LRU_WIDTH), LRU_CONV ** -0.5),
        'lru_conv_b': nrm(ks[9], (L, LRU_WIDTH), 0.02),
        'lru_wa': nrm(ks[10], (L, LRU_BLOCKS, bw, bw), bw ** -0.5),
        'lru_ba': nrm(ks[11], (L, LRU_WIDTH), 0.02),
        'lru_wx': nrm(ks[12], (L, LRU_BLOCKS, bw, bw), bw ** -0.5),
        'lru_bx': nrm(ks[13], (L, LRU_WIDTH), 0.02),
        'lru_lambda': jnp.log(a0) - jnp.log1p(-a0),
        'w_out': nrm(ks[15], (L, MIX_WIDTH, D_MODEL), MIX_WIDTH ** -0.5),
        'norm_mem': 1.0 + nrm(ks[16], (L, D_MODEL), 0.1),
        'norm_memkv': 1.0 + nrm(ks[17], (L, D_MODEL), 0.1),
        'w_q_mem': nrm(ks[18], (L, D_MODEL, D_MODEL), D_MODEL ** -0.5),
        'w_kv_mem': nrm(ks[19], (L, D_MODEL, 2 * D_MODEL), D_MODEL ** -0.5),
        'mq_gain': 1.0 + nrm(ks[20], (L, MEM_HEAD_DIM), 0.1),
        'mk_gain': 1.0 + nrm(ks[21], (L, MEM_HEAD_DIM), 0.1),
        'w_o_mem': nrm(ks[22], (L, D_MODEL, D_MODEL), D_MODEL ** -0.5),
        'norm_ffn': 1.0 + nrm(ks[23], (L, D_MODEL), 0.1),
        'w_up': nrm(ks[24], (L, D_MODEL, 2 * D_FF), D_MODEL ** -0.5),
        'ffn_conv_w': nrm(ks[25], (L, FFN_CONV, D_FF), FFN_CONV ** -0.5),
        'ffn_conv_b': nrm(ks[26], (L, D_FF), 0.02),
        'w_down': nrm(ks[27], (L, D_FF, D_MODEL), D_FF ** -0.5),
    }


def reference(x, mem, norm_mix, w_in, pool_w, pool_scale, q_gain, k_gain, lru_conv_w, lru_conv_b,
              lru_wa, lru_ba, lru_wx, lru_bx, lru_lambda, w_out, norm_mem, norm_memkv, w_q_mem,
              w_kv_mem, mq_gain, mk_gain, w_o_mem, norm_ffn, w_up, ffn_conv_w, ffn_conv_b, w_down):
    slopes = alibi_slopes(ATTN_HEADS)
    h = x
    for l in range(DEPTH):
        h = h + mixer_block(rmsnorm(h, norm_mix[l]), w_in[l], pool_w[l], pool_scale[l], q_gain[l], k_gain[l],
                            lru_conv_w[l], lru_conv_b[l], lru_wa[l], lru_ba[l], lru_wx[l], lru_bx[l],
                            lru_lambda[l], w_out[l], slopes)
        h = h + memory_attn(rmsnorm(h, norm_mem[l]), rmsnorm(mem, norm_memkv[l]), w_q_mem[l], w_kv_mem[l],
                            mq_gain[l], mk_gain[l], w_o_mem[l])
        h = h + conv_ffn(rmsnorm(h, norm_ffn[l]), w_up[l], ffn_conv_w[l], ffn_conv_b[l], w_down[l])
    return h
```

```python
import bisect
from contextlib import ExitStack

import numpy as np
import concourse.bass as bass
import concourse.mybir as mybir
from concourse.bass_utils import run_bass_kernel_spmd

F32 = mybir.dt.float32
BF16 = mybir.dt.bfloat16
AF = mybir.ActivationFunctionType
ALU = mybir.AluOpType

D = 1024
SEQ = 2048
DEPTH = 4
NCORE = 8
MEM = 256
DFF = 2816
INW = 2176
EPS = 1e-6
TT = 512
NTT = SEQ // TT
PATTERNS = ((128, 1), (512, 4), (2048, 16))
NSLOT = 5
SLOT_E = 2048
LOOKAHEAD = 3
FFN_PARTS = ((0, 8), (8, 15), (15, 22))
SEM_LIMIT = 30000


def _I(m, *a, **k):
    return (m, a, k)


class _Op:
    __slots__ = ("eng", "fn", "deps", "sig", "semi", "val", "dma", "dsem", "dval", "idx")


class Prog:
    ENGS = ("pe", "act", "dve", "pool", "sp")

    def __init__(self, nc, es):
        self.nc = nc
        self.es = es
        self.ops = []
        self.lastw = {}
        self.readers = {}
        self.dma_sems = {}
        self.n_sems = 0

    def _new_sem(self, name):
        self.n_sems += 1
        return self.es.enter_context(self.nc.semaphore(name))

    def op(self, eng, fn, reads=(), writes=(), dma=None):
        o = _Op()
        o.eng = eng; o.fn = fn; o.sig = False; o.dma = dma; o.idx = len(self.ops)
        deps = set()
        lastw = self.lastw; readers = self.readers
        for k in reads:
            w = lastw.get(k)
            if w is not None:
                deps.add(w)
        for k in writes:
            w = lastw.get(k)
            if w is not None:
                deps.add(w)
            rl = readers.get(k)
            if rl:
                deps.update(rl)
        if eng == "pe" and dma is None:
            deps = {d for d in deps if not (d.eng == "pe" and d.dma is None)}
        o.deps = deps
        for d in deps:
            d.sig = True
        for k in writes:
            lastw[k] = o
            readers[k] = []
        for k in reads:
            rl = readers.get(k)
            if rl is None:
                readers[k] = [o]
            else:
                rl.append(o)
        self.ops.append(o)
        return o

    def pe(self, fn, r=(), w=()): return self.op("pe", fn, r, w)
    def act(self, fn, r=(), w=()): return self.op("act", fn, r, w)
    def dve(self, fn, r=(), w=()): return self.op("dve", fn, r, w)
    def pool(self, fn, r=(), w=()): return self.op("pool", fn, r, w)
    def dma(self, fn, sem, r=(), w=(), q="sp"): return self.op(q, fn, r, w, dma=sem)

    def emit(self):
        nc = self.nc
        cur = {}
        for o in self.ops:
            if o.dma is not None:
                ent = self.dma_sems.get(o.dma)
                if ent is None:
                    ent = [self._new_sem("d_" + o.dma), 0]
                    self.dma_sems[o.dma] = ent
                ent[1] += 16
                o.dsem = ent[0]; o.dval = ent[1]
                continue
            if not o.sig:
                continue
            ent = cur.get(o.eng)
            if ent is None or ent[1] >= SEM_LIMIT:
                ent = [self._new_sem("e_%s_%d" % (o.eng, self.n_sems)), 0]
                cur[o.eng] = ent
            ent[1] += 1
            o.semi = ent[0]; o.val = ent[1]
        dma_hist = {}
        for o in self.ops:
            if o.dma is not None:
                dma_hist.setdefault(o.dma, []).append((o.idx, o.dval))
        dma_idx = {k: [a for a, _ in v] for k, v in dma_hist.items()}
        per_eng = {e: [] for e in self.ENGS}
        for o in self.ops:
            per_eng[o.eng].append(o)

        def run(engname, e):
            waited = {}
            for o in per_eng[engname]:
                need = {}
                for d in o.deps:
                    if d.dma is not None:
                        j = bisect.bisect_left(dma_idx[d.dma], o.idx) - 1
                        sem, val = d.dsem, dma_hist[d.dma][j][1]
                    else:
                        sem, val = d.semi, d.val
                    key = id(sem)
                    if key not in need or need[key][1] < val:
                        need[key] = (sem, val)
                for key, (sem, val) in need.items():
                    if waited.get(key, 0) < val:
                        e.wait_ge(sem, val)
                        waited[key] = val
                f = o.fn
                ins = getattr(e, f[0])(*f[1], **f[2]) if isinstance(f, tuple) else f(e)
                if ins is None:
                    continue
                if o.dma is not None:
                    ins.then_inc(o.dsem, 16)
                elif o.sig:
                    ins.then_inc(o.semi, 1)

        with nc.Block() as block:
            @block.tensor
            def _(e): run("pe", e)

            @block.scalar
            def _(e): run("act", e)

            @block.vector
            def _(e): run("dve", e)

            @block.gpsimd
            def _(e): run("pool", e)

            @block.sync
            def _(e): run("sp", e)


def _piece(W, rows, cols):
    blk = np.stack([np.concatenate([W[r:r + 128, c:c + 128] for c in cols], axis=1) for r in rows], axis=1)
    return blk.reshape(128, -1)


def weight_pieces_layer(inp, l):
    w_in = inp["w_in"][l]; w_out = inp["w_out"][l]; w_q = inp["w_q_mem"][l]; w_kv = inp["w_kv_mem"][l]
    w_o = inp["w_o_mem"][l]; w_up = inp["w_up"][l]; w_down = inp["w_down"][l]
    k8 = [i * 128 for i in range(8)]
    out = []
    for j in range(3):
        out.append(("lru%d" % j, 8, 256, _piece(w_in, k8, [1408 + j * 128, 1792 + j * 128])))
    out.append(("pool", 8, 256, _piece(w_in, k8, [0, 128])))
    rowsA = [640, 768, 896, 0, 128]
    for i in range(4):
        out.append(("woA%d" % i, 5, 256, _piece(w_out, rowsA, [i * 256, i * 256 + 128])))
    qkv_cols = [256 + i * 128 for i in range(9)]
    for i in range(0, 9, 2):
        cs = qkv_cols[i:i + 2]
        out.append(("qkv%d" % (i // 2), 8, 128 * len(cs), _piece(w_in, k8, cs)))
    rowsB = [256, 384, 512]
    for i in range(4):
        out.append(("woB%d" % i, 3, 256, _piece(w_out, rowsB, [i * 256, i * 256 + 128])))
    for i in range(4):
        out.append(("mk%d" % i, 8, 256, _piece(w_kv, k8, [i * 256, i * 256 + 128])))
    for i in range(4):
        out.append(("mv%d" % i, 8, 256, _piece(w_kv, k8, [1024 + i * 256, 1024 + i * 256 + 128])))
    for i in range(4):
        out.append(("mq%d" % i, 8, 256, _piece(w_q, k8, [i * 256, i * 256 + 128])))
    for i in range(4):
        out.append(("mo%d" % i, 8, 256, _piece(w_o, k8, [i * 256, i * 256 + 128])))
    for (a, b) in FFN_PARTS:
        for j in range(a, b):
            out.append(("up%d" % j, 8, 256, _piece(w_up, k8, [j * 128, DFF + j * 128])))
        rows = [j * 128 for j in range(a, b)]
        for i in range(4):
            out.append(("dn%d_%d" % (a, i), b - a, 256, _piece(w_down, rows, [i * 256, i * 256 + 128])))
    return out


class SP:
    names = [("g_mix", 8), ("g_mem", 8), ("g_memkv", 8), ("g_ffn", 8), ("pool_scale", 2), ("q_gain", 1), ("k_gain", 1),
             ("lcw", 12), ("lcb", 3), ("lba", 3), ("lbx", 3), ("llam", 3), ("mqg", 2), ("mkg", 2),
             ("fcw", 66), ("fcb", 22)]
    off = {}
    n = 0
    for _nm, _w in names:
        off[_nm] = n
        n += _w
    PER_LAYER = n


def fm(v, nchunk):
    return np.ascontiguousarray(v.reshape(nchunk, 128).T)


def small_params(inp):
    t = np.zeros((128, DEPTH, SP.PER_LAYER), np.float32)
    for l in range(DEPTH):
        def put(name, arr):
            o = SP.off[name]
            t[:, l, o:o + arr.shape[1]] = arr
        put("g_mix", fm(inp["norm_mix"][l], 8)); put("g_mem", fm(inp["norm_mem"][l], 8))
        put("g_memkv", fm(inp["norm_memkv"][l], 8)); put("g_ffn", fm(inp["norm_ffn"][l], 8))
        put("pool_scale", fm(inp["pool_scale"][l], 2))
        put("q_gain", np.tile(inp["q_gain"][l], 2)[:, None]); put("k_gain", np.tile(inp["k_gain"][l], 2)[:, None])
        put("lcw", np.concatenate([fm(inp["lru_conv_w"][l, k], 3) for k in range(4)], axis=1))
        put("lcb", fm(inp["lru_conv_b"][l], 3)); put("lba", fm(inp["lru_ba"][l], 3)); put("lbx", fm(inp["lru_bx"][l], 3))
        put("llam", fm(inp["lru_lambda"][l], 3))
        put("mqg", fm(inp["mq_gain"][l], 2)); put("mkg", fm(inp["mk_gain"][l], 2))
        put("fcw", np.concatenate([fm(inp["ffn_conv_w"][l, k], 22) for k in range(3)], axis=1))
        put("fcb", fm(inp["ffn_conv_b"][l], 22))
    return t.reshape(128, DEPTH * SP.PER_LAYER)


def blockdiag_params(inp):
    t = np.zeros((128, DEPTH, 8, 128), np.float32)
    for l in range(DEPTH):
        for c in range(2):
            for hh in range(2):
                t[hh * 64:(hh + 1) * 64, l, c, hh * 64:(hh + 1) * 64] = inp["pool_w"][l, 2 * c + hh]
        for c in range(3):
            for hh in range(2):
                t[hh * 64:(hh + 1) * 64, l, 2 + c, hh * 64:(hh + 1) * 64] = inp["lru_wa"][l, 2 * c + hh]
                t[hh * 64:(hh + 1) * 64, l, 5 + c, hh * 64:(hh + 1) * 64] = inp["lru_wx"][l, 2 * c + hh]
    return t.reshape(128, DEPTH * 8 * 128)


def const_tables():
    slopes = np.array([2.0 ** (-8.0 * (h + 1) / 6) for h in range(6)], np.float64)
    j = np.arange(128)[:, None].astype(np.float64); i = np.arange(128)[None, :].astype(np.float64)
    E = np.zeros((128, 18, 2, 128), np.float64)
    for p, (w, d) in enumerate(PATTERNS):
        for h in range(6):
            S = slopes[h] * d
            dist_c = i - j
            E[:, p * 6 + h, 0, :] = np.where(dist_c >= 0, np.exp(-S * np.maximum(dist_c, 0)), 0.0)
            dist_p = 128 + i - j
            E[:, p * 6 + h, 1, :] = np.where(dist_p <= 128, np.exp(-S * dist_p), 0.0)
    invc = np.zeros((128, 2, 16), np.float64); invw = np.zeros((128, 2), np.float64)
    wins = ((2, 4), (8, 16))
    t = np.arange(16)
    for c in range(2):
        for hh in range(2):
            w = wins[c][hh]
            invc[hh * 64:(hh + 1) * 64, c, :] = 1.0 / np.minimum(t + 1, w)
            invw[hh * 64:(hh + 1) * 64, c] = 1.0 / w
    cst = np.concatenate([invc.reshape(128, 32), invw], axis=1).astype(np.float32)
    return E.reshape(128, 18 * 256).astype(np.float32), cst


def build_program(nseq=4, nlayers=DEPTH, piece_meta=None, debug_stage=None):
    nc = bass.Bass("TRN2", target_bir_lowering=False)
    per_layer_elems = sum(k * n for _, k, n in piece_meta)
    xT = nc.dram_tensor("xT", [nseq, D, SEQ], F32, kind="ExternalInput").ap()
    memT = nc.dram_tensor("memT", [nseq, D, MEM], F32, kind="ExternalInput").ap()
    wst = nc.dram_tensor("wst", [128, nlayers * per_layer_elems], F32, kind="ExternalInput").ap()
    spd = nc.dram_tensor("spd", [128, DEPTH * SP.PER_LAYER], F32, kind="ExternalInput").ap()
    bdd = nc.dram_tensor("bdd", [128, DEPTH * 8 * 128], F32, kind="ExternalInput").ap()
    Ed = nc.dram_tensor("Ed", [128, 18 * 256], F32, kind="ExternalInput").ap()
    cstd = nc.dram_tensor("cstd", [128, 34], F32, kind="ExternalInput").ap()
    outT = nc.dram_tensor("outT", [nseq, D, SEQ], F32, kind="ExternalOutput").ap()

    es = ExitStack()
    with es:
        P = Prog(nc, es)
        sb = lambda name, shape, dt: es.enter_context(nc.sbuf_tensor(name, shape, dt))
        h = sb("h", [128, 8, SEQ], F32)
        hn = sb("hn", [128, 8, SEQ], BF16)
        big = sb("big", [128, 9, SEQ], BF16)
        wsl = sb("wsl", [128, NSLOT, SLOT_E], BF16)
        scr = sb("scr", [128, 12, TT], F32)
        Eb = sb("Eb", [128, 18, 256], BF16)
        spt = sb("spt", [128, DEPTH, SP.PER_LAYER], F32)
        bdt = sb("bdt", [128, 1, 8, 128], BF16)
        cst = sb("cst", [128, 34], F32)
        dummy = sb("dummyt", [128, 2], F32)
        der = sb("der", [128, 8], F32)
        ones = sb("ones", [128, 128], BF16)
        bones = sb("bones", [128, 128], BF16)
        ident = sb("ident", [128, 128], BF16)
        sqb = sb("sqb", [128, 2, TT], BF16)
        rsb = sb("rsb", [128, 2, TT], F32)
        smallb = sb("smallb", [128, 6, TT], BF16)
        idf = scr[:, 11, 0:128]
        memn = smallb[:, 0:4, :].rearrange("p a b -> p (a b)").rearrange("p (c m) -> p c m", c=8)
        kn = scr[:, 4:6, :].rearrange("p a b -> p (a b)").bitcast(BF16).rearrange("p (c m) -> p c m", c=8)
        vmem = scr[:, 6:8, :].rearrange("p a b -> p (a b)").bitcast(BF16).rearrange("p (c m) -> p c m", c=2)
        psb = [es.enter_context(nc.psum_tensor("ps%d" % i, [128, TT], F32)) for i in range(8)]

        st = {"bank": 0}

        def bank():
            b = st["bank"]
            st["bank"] = (b + 1) % 8
            return b

        def PK(b): return ("ps", b)

        stream = []
        for s in range(nseq):
            for l in range(nlayers):
                off = l * per_layer_elems
                for i, (nm, kc, n) in enumerate(piece_meta):
                    stream.append((nm, off, kc, n))
                    off += kc * n
        wstate = {"issued": 0, "next": 0}

        def w_issue_upto(k):
            while wstate["issued"] < min(k, len(stream)):
                i = wstate["issued"]
                nm, off, kc, n = stream[i]
                slot = i % NSLOT
                src = wst[:, off:off + kc * n]
                dst = wsl[:, slot, 0:kc * n]
                P.dma(_I("dma_start", out=dst, in_=src), "w%d" % slot, w=[("w", slot)], q="pool")
                wstate["issued"] += 1

        def w_next(expect):
            i = wstate["next"]
            nm, off, kc, n = stream[i]
            assert nm == expect, (nm, expect)
            w_issue_upto(i + 1 + LOOKAHEAD)
            wstate["next"] += 1
            slot = i % NSLOT
            return wsl[:, slot, 0:kc * n].rearrange("p (k n) -> p k n", k=kc), ("w", slot)

        P.dma(_I("dma_start", out=spt[:].rearrange("p a b -> p (a b)"), in_=spd), "c0", w=["spt"])
        P.dma(_I("dma_start", out=cst[:], in_=cstd), "c1", w=["cst"])
        P.dma(_I("dma_start", out=Eb[:].rearrange("p a b -> p (a b)"), in_=Ed), "c2", w=["Eb"], q="pool")
        P.dve(_I("memset", ones[:], 1.0), w=["ones"])
        P.dve(_I("memset", bones[:], 0.0), w=["bones"])
        P.dve(_I("memset", bones[0:64, 0:64], 1.0), w=["bones"])
        P.dve(_I("memset", bones[64:128, 64:128], 1.0), w=["bones"])
        P.pool(_I("memset", idf[:], 1.0), w=["idf"])
        P.pool(_I("affine_select", out=idf[:], in_=idf[:], pattern=[[-1, 128]], compare_op=ALU.is_equal,
                                         fill=0.0, base=0, channel_multiplier=1), r=["idf"], w=["idf"])
        P.dve(_I("tensor_copy", out=ident[:], in_=idf[:]), r=["idf"], w=["ident"])

        def spc(l, name, j=0, w=1):
            o = SP.off[name] + j
            return spt[:, l, o:o + w]

        def mm_group(b, lhs, rhs, rkeys, N=TT, M=128, po=0, co=0, tp=None):
            ps = psb[b][po:po + M, co:co + N]
            n = len(lhs)
            for i in range(n):
                if tp is None:
                    P.pe(_I("matmul", ps, lhs[i], rhs[i], start=(i == 0), stop=(i == n - 1)), r=rkeys, w=[PK(b)])
                else:
                    P.pe(_I("matmul", ps, lhs[i], rhs[i], start=(i == 0), stop=(i == n - 1), tile_position=tp), r=rkeys, w=[PK(b)])

        def rstd_from_ps(b, N, inv_n, slot):
            rs = rsb[:, slot, 0:N]
            P.act(_I("activation", out=rs, in_=psb[b][:, 0:N], func=AF.Sqrt, bias=EPS, scale=inv_n),
                  r=[PK(b)], w=[("rsb", slot)])
            P.dve(_I("reciprocal", out=rs, in_=rs), r=[("rsb", slot)], w=[("rsb", slot)])
            return rs

        def rmsnorm_h(l, gname):
            for tt in range(NTT):
                ts = slice(tt * TT, (tt + 1) * TT)
                b = bank()
                for c in range(8):
                    sq = sqb[:, c % 2, :]
                    P.act(_I("activation", out=sq, in_=h[:, c, ts], func=AF.Square),
                          r=[("h", c, tt)], w=[("sqb", c % 2)])
                    P.pe(_I("matmul", psb[b][:, :], ones[:], sq, start=(c == 0), stop=(c == 7)),
                         r=[("sqb", c % 2), "ones"], w=[PK(b)])
                slot = tt % 2
                rs = rstd_from_ps(b, TT, 1.0 / D, slot)
                for c in range(8):
                    P.dve(_I("scalar_tensor_tensor", out=hn[:, c, ts], in0=h[:, c, ts], scalar=spc(l, gname, c),
                                                                      in1=rs, op0=ALU.mult, op1=ALU.mult),
                          r=[("h", c, tt), ("rsb", slot), "spt"], w=[("hn", c, tt)])

        def hn_keys(tt): return [("hn", c, tt) for c in range(8)]

        def resid_add(b, oc, tt):
            ts = slice(tt * TT, (tt + 1) * TT)
            P.dve(_I("tensor_tensor", out=h[:, oc, ts], in0=psb[b][:, :], in1=h[:, oc, ts], op=ALU.add),
                  r=[PK(b), ("h", oc, tt)], w=[("h", oc, tt)])

        def S(i): return scr[:, i, :]

        def SK(i): return ("scr", i)

        def layer_setup(l, par):
            src = bdd[:, l * 1024:(l + 1) * 1024]
            P.dma(_I("dma_start", out=bdt[:, par, :, :].rearrange("p a b -> p (a b)"), in_=src), "bd%d" % par,
                  w=[("bdt", par)], q="pool")
            P.act(_I("activation", out=der[:, 0:3], in_=spc(l, "llam", 0, 3), func=AF.Exp, scale=-1.0),
                  r=["spt"], w=["der"])
            P.act(_I("activation", out=der[:, 0:3], in_=der[:, 0:3], func=AF.Ln, bias=1.0), r=["der"], w=["der"])
            P.dve(_I("tensor_scalar", out=der[:, 0:3], in0=der[:, 0:3], scalar1=-8.0, scalar2=None, op0=ALU.mult),
                  r=["der"], w=["der"])
            P.dve(_I("tensor_scalar", out=der[:, 3:4], in0=spc(l, "q_gain"), scalar1=0.125, scalar2=None, op0=ALU.mult),
                  r=["spt", "der"], w=["der"])
            P.dve(_I("tensor_scalar", out=der[:, 4:6], in0=spc(l, "mqg", 0, 2), scalar1=1.0 / 16.0, scalar2=None,
                                            op0=ALU.mult), r=["spt", "der"], w=["der"])

        def stage_lru(l, par):
            xbuf = scr[:, 0:2, :].rearrange("p a b -> p (a b)")
            units = [(j, tt) for j in range(3) for tt in range(NTT)]
            wps = {}

            def a_unit(n):
                j, tt = units[n]
                if tt == 0:
                    wps[j] = w_next("lru%d" % j)
                    P.dve(_I("memset", xbuf[:, 0:3], 0.0), w=[SK(0)])
                wp, wk = wps[j]
                ts = slice(tt * TT, (tt + 1) * TT)
                gs = 2 if n % 2 == 0 else 10
                xs = 3 if n % 2 == 0 else 11
                bx = bank(); by = bank()
                mm_group(bx, [wp[:, kc, 0:128] for kc in range(8)], [hn[:, kc, ts] for kc in range(8)], [wk] + hn_keys(tt))
                mm_group(by, [wp[:, kc, 128:256] for kc in range(8)], [hn[:, kc, ts] for kc in range(8)], [wk] + hn_keys(tt))
                P.act(_I("activation", out=xbuf[:, 3:3 + TT], in_=psb[bx][:, :], func=AF.Identity), r=[PK(bx)], w=[SK(0), SK(1)])
                P.act(_I("activation", out=S(gs), in_=psb[by][:, :], func=AF.Gelu_apprx_tanh), r=[PK(by)], w=[SK(gs)])
                xc = S(xs)
                P.dve(_I("tensor_scalar", out=xc, in0=xbuf[:, 3:3 + TT], scalar1=spc(l, "lcw", 0 * 3 + j),
                         scalar2=spc(l, "lcb", j), op0=ALU.mult, op1=ALU.add), r=[SK(0), SK(1), "spt"], w=[SK(xs)])
                for k in range(1, 4):
                    P.dve(_I("scalar_tensor_tensor", out=xc, in0=xbuf[:, 3 - k:3 - k + TT], scalar=spc(l, "lcw", k * 3 + j),
                             in1=xc, op0=ALU.mult, op1=ALU.add), r=[SK(0), SK(1), SK(xs), "spt"], w=[SK(xs)])
                P.dve(_I("tensor_copy", out=xbuf[:, 0:3], in_=xbuf[:, TT:TT + 3]), r=[SK(0), SK(1)], w=[SK(0)])
                P.act(_I("activation", out=smallb[:, n % 2, :], in_=xc, func=AF.Identity), r=[SK(xs)], w=[("smb", n % 2)])

            def b_unit(n):
                j, tt = units[n]
                ts = slice(tt * TT, (tt + 1) * TT)
                gs = 2 if n % 2 == 0 else 10
                xs = 3 if n % 2 == 0 else 11
                xc = S(xs); gy = S(gs); xcb = smallb[:, n % 2, :]
                ba_ = bank(); bxg = bank()
                mm_group(ba_, [bdt[:, par, 2 + j, :]], [xcb], [("bdt", par), ("smb", n % 2)])
                mm_group(bxg, [bdt[:, par, 5 + j, :]], [xcb], [("bdt", par), ("smb", n % 2)])
                rg = S(4); ig = S(5); aa = S(6); t1 = S(7)
                P.act(_I("activation", out=rg, in_=psb[ba_][:, :], func=AF.Sigmoid, bias=spc(l, "lba", j)), r=[PK(ba_), "spt"], w=[SK(4)])
                P.act(_I("activation", out=ig, in_=psb[bxg][:, :], func=AF.Sigmoid, bias=spc(l, "lbx", j)), r=[PK(bxg), "spt"], w=[SK(5)])
                P.act(_I("activation", out=aa, in_=rg, func=AF.Exp, scale=der[:, j:j + 1]), r=[SK(4), "der"], w=[SK(6)])
                P.act(_I("activation", out=t1, in_=aa, func=AF.Square), r=[SK(6)], w=[SK(7)])
                P.act(_I("activation", out=t1, in_=t1, func=AF.Sqrt, bias=1.0, scale=-1.0), r=[SK(7)], w=[SK(7)])
                P.dve(_I("tensor_tensor", out=ig, in0=ig, in1=xc, op=ALU.mult), r=[SK(5), SK(xs)], w=[SK(5)])
                P.dve(_I("tensor_tensor", out=ig, in0=ig, in1=t1, op=ALU.mult), r=[SK(5), SK(7)], w=[SK(5)])
                hr = S(8 + (tt % 2))
                if tt == 0:
                    P.dve(_I("tensor_tensor_scan", out=hr, data0=aa, data1=ig, initial=0.0, op0=ALU.mult, op1=ALU.add),
                          r=[SK(6), SK(5)], w=[SK(8 + (tt % 2))])
                else:
                    prev = S(8 + ((tt - 1) % 2))
                    P.dve(_I("tensor_tensor_scan", out=hr, data0=aa, data1=ig, initial=prev[:, TT - 1:TT], op0=ALU.mult,
                             op1=ALU.add), r=[SK(6), SK(5), SK(8 + ((tt - 1) % 2))], w=[SK(8 + (tt % 2))])
                P.dve(_I("tensor_tensor", out=big[:, j, ts], in0=hr, in1=gy, op=ALU.mult),
                      r=[SK(8 + (tt % 2)), SK(gs)], w=[("big", j, tt)])

            N_ = len(units)
            a_unit(0)
            for n in range(N_):
                if n + 1 < N_:
                    a_unit(n + 1)
                b_unit(n)

        def stage_pool(l, par):
            wp, wk = w_next("pool")
            u = scr[:, 0:4, :].rearrange("p a b -> p (a b)")
            sA = scr[:, 4:8, :].rearrange("p a b -> p (a b)")
            sB = scr[:, 8:12, :].rearrange("p a b -> p (a b)")
            UK = [SK(i) for i in range(0, 4)]; AK = [SK(i) for i in range(4, 8)]; BK = [SK(i) for i in range(8, 12)]

            def shadd(dst, src, sh, lo, hi, dk, sk):
                P.dve(_I("tensor_copy", out=dst[lo:hi, 0:sh], in_=src[lo:hi, 0:sh]), r=sk, w=dk)
                P.dve(_I("tensor_tensor", out=dst[lo:hi, sh:SEQ], in0=src[lo:hi, sh:SEQ], in1=src[lo:hi, 0:SEQ - sh],
                                                op=ALU.add), r=sk, w=dk)

            for c in range(2):
                for tt in range(NTT):
                    ts = slice(tt * TT, (tt + 1) * TT)
                    b = bank()
                    mm_group(b, [wp[:, kc, c * 128:(c + 1) * 128] for kc in range(8)], [hn[:, kc, ts] for kc in range(8)],
                             [wk] + hn_keys(tt))
                    P.act(_I("activation", out=u[:, ts], in_=psb[b][:, :], func=AF.Identity),
                          r=[PK(b)], w=[SK(tt)])
                shadd(sA, u, 1, 0, 128, AK, UK)
                if c == 0:
                    shadd(sB, sA, 2, 64, 128, BK, AK)
                else:
                    shadd(sB, sA, 2, 0, 128, BK, AK)
                    shadd(sA, sB, 4, 0, 128, AK, BK)
                    shadd(sB, sA, 8, 64, 128, BK, AK)
                pl = big[:, 3 + c, :]
                plk = [("big", 3 + c, tt) for tt in range(NTT)]
                for (lo, hi, F, fk) in ((0, 64, sA, AK), (64, 128, sB, BK)):
                    P.dve(_I("scalar_tensor_tensor",
                        out=pl[lo:hi, 16:SEQ], in0=F[lo:hi, 16:SEQ], scalar=cst[lo:hi, 32 + c:33 + c], in1=u[lo:hi, 16:SEQ],
                        op0=ALU.mult, op1=ALU.subtract), r=fk + UK + ["cst"], w=plk)
                    P.dve(_I("tensor_tensor", out=F[lo:hi, 0:16], in0=F[lo:hi, 0:16],
                                                                        in1=cst[lo:hi, c * 16:(c + 1) * 16], op=ALU.mult),
                          r=fk + ["cst"], w=fk)
                    P.dve(_I("tensor_tensor", out=pl[lo:hi, 0:16], in0=F[lo:hi, 0:16],
                                                                        in1=u[lo:hi, 0:16], op=ALU.subtract),
                          r=fk + UK, w=plk)
                for tt in range(NTT):
                    ts = slice(tt * TT, (tt + 1) * TT)
                    b = bank()
                    mm_group(b, [bdt[:, par, c, :]], [big[:, 3 + c, ts]], [("bdt", par), ("big", 3 + c, tt)])
                    P.act(_I("activation", out=big[:, 3 + c, ts], in_=psb[b][:, :], func=AF.Identity,
                                                             scale=spc(l, "pool_scale", c)),
                          r=[PK(b), "spt"], w=[("big", 3 + c, tt)])

        def stage_wout(name, srcs):
            K = len(srcs)
            for i in range(4):
                wp, wk = w_next("%s%d" % (name, i))
                for tt in range(NTT):
                    ts = slice(tt * TT, (tt + 1) * TT)
                    for m in range(2):
                        b = bank()
                        mm_group(b, [wp[:, k, m * 128:(m + 1) * 128] for k in range(K)], [big[:, srcs[k], ts] for k in range(K)],
                                 [wk] + [("big", srcs[k], tt) for k in range(K)])
                        resid_add(b, 2 * i + m, tt)

        def stage_qkv(l):
            pend = []

            def tail(idx, tt, k2):
                ts = slice(tt * TT, (tt + 1) * TT)
                sq = sqb[:, k2 % 2, :]
                qf = S(k2 % 4)
                b2 = bank()
                mm_group(b2, [bones[:]], [sq], ["bones", ("sqb", k2 % 2)])
                rs = rstd_from_ps(b2, TT, 1.0 / 64.0, k2 % 2)
                gsc = der[:, 3:4] if idx < 3 else spc(l, "k_gain")
                P.dve(_I("scalar_tensor_tensor", out=big[:, idx, ts], in0=qf, scalar=gsc, in1=rs, op0=ALU.mult, op1=ALU.mult),
                      r=[SK(k2 % 4), ("rsb", k2 % 2), "der", "spt"], w=[("big", idx, tt)])

            for pi in range(5):
                wp, wk = w_next("qkv%d" % pi)
                ncb = 2 if pi < 4 else 1
                for cb in range(ncb):
                    idx = pi * 2 + cb
                    for tt in range(NTT):
                        ts = slice(tt * TT, (tt + 1) * TT)
                        b = bank()
                        mm_group(b, [wp[:, kc, cb * 128:(cb + 1) * 128] for kc in range(8)], [hn[:, kc, ts] for kc in range(8)],
                                 [wk] + hn_keys(tt))
                        while pend:
                            tail(*pend.pop(0))
                        if idx >= 6:
                            P.act(_I("activation", out=big[:, idx, ts], in_=psb[b][:, :], func=AF.Identity),
                                  r=[PK(b)], w=[("big", idx, tt)])
                            continue
                        k2 = st.setdefault("qk", 0); st["qk"] = k2 + 1
                        P.act(_I("activation", out=sqb[:, k2 % 2, :], in_=psb[b][:, :], func=AF.Square), r=[PK(b)], w=[("sqb", k2 % 2)])
                        P.act(_I("activation", out=S(k2 % 4), in_=psb[b][:, :], func=AF.Identity), r=[PK(b)], w=[SK(k2 % 4)])
                        pend.append((idx, tt, k2))
            while pend:
                tail(*pend.pop(0))

        def stage_attn(l):
            acc = scr[:, 0:8, :].rearrange("p (a b) c -> p a (b c)", a=2)
            ACCK = [SK(i) for i in range(8)]
            DEPTH_A = 2
            PTS = (0, 1, 5)
            SUBK = ([("smb", a, hh) for a in PTS for hh in range(2)] + [(SK(a), hh) for a in range(9, 12) for hh in range(2)]
                    + [("smbv", q) for q in range(4)])
            WHOLEK = [("smb", a) for a in (0, 1, 2, 3, 5)] + [SK(9), SK(10), SK(11)]
            P.dve(_I("memset", dummy[:, 0:1], 0.0), w=WHOLEK + SUBK + ["dummy"])

            def vtile(q):
                return smallb[:, 2 + q // 2, (q % 2) * 256:(q % 2) * 256 + 256]
            for q in range(4):
                P.dve(_I("memset", vtile(q)[:, 64:128], 1.0), w=[("smbv", q)])
                P.dve(_I("memset", vtile(q)[:, 192:256], 1.0), w=[("smbv", q)])
            for c in range(3):
                qk_ = [("big", c, tt) for tt in range(NTT)]
                kk_ = [("big", 3 + c, tt) for tt in range(NTT)]
                vk_ = [("big", 6 + c, tt) for tt in range(NTT)]
                its = []
                for p, (w, d) in enumerate(PATTERNS):
                    nb = (SEQ // d) // 128
                    for r in range(d):
                        for bq in range(nb):
                            its.append((p, d, r, bq))

                def stage_a(n):
                    p, d, r, bq = its[n]
                    s0 = bq * 128 * d + r
                    tok = slice(s0, s0 + 127 * d + 1, d)
                    q4 = n % 4
                    vt = vtile(q4)
                    bvb = 4 + n % 2
                    pv_ = psb[bvb][:, 0:128]
                    P.pe(_I("matmul", pv_, big[:, 6 + c, tok], ident[:], start=True, stop=True), r=vk_ + ["ident"], w=[PK(bvb)])
                    P.act(_I("activation", out=vt.rearrange("p (a b) -> p a b", a=2)[:, :, 0:64],
                             in_=pv_.rearrange("p (a b) -> p a b", a=2), func=AF.Identity), r=[PK(bvb)], w=[("smbv", q4)])
                    W_ = 256 if bq > 0 else 128
                    pslot = 9 + (n % 3)
                    pe_ = S(pslot).bitcast(BF16)[:, 0:512]
                    ptslot = PTS[n % 3]
                    pT = smallb[:, ptslot, :]
                    for hh in range(2):
                        rows = slice(hh * 64, (hh + 1) * 64)
                        sb_ = 2 * (n % 2) + hh
                        sc = psb[sb_][:, 0:256]
                        P.pe(_I("matmul", sc[:, 0:128], big[rows, 3 + c, tok], big[rows, c, tok], start=True, stop=True),
                             r=qk_ + kk_, w=[PK(sb_)])
                        if bq > 0:
                            tokp = slice(s0 - 128 * d, s0 - d + 1, d)
                            P.pe(_I("matmul", sc[:, 128:256], big[rows, 3 + c, tokp], big[rows, c, tok], start=True, stop=True),
                                 r=qk_ + kk_, w=[PK(sb_)])
                        P.act(_I("activation", out=pe_[:, hh * 256:hh * 256 + W_], in_=sc[:, 0:W_], func=AF.Exp),
                              r=[PK(sb_)], w=[(SK(pslot), hh)])
                    P.dve(_I("tensor_tensor", out=pT.rearrange("p (a b) -> p a b", a=2)[:, :, 0:W_],
                             in0=pe_.rearrange("p (a b) -> p a b", a=2)[:, :, 0:W_],
                             in1=Eb[:, p * 6 + 2 * c:p * 6 + 2 * c + 2, 0:W_], op=ALU.mult),
                          r=[(SK(pslot), 0), (SK(pslot), 1), "Eb"], w=[("smb", ptslot, 0), ("smb", ptslot, 1)])

                def stage_b(n):
                    p, d, r, bq = its[n]
                    s0 = bq * 128 * d + r
                    tok = slice(s0, s0 + 127 * d + 1, d)
                    nblk = 2 if bq > 0 else 1
                    q4 = n % 4
                    pq4 = (n - 1) % 4
                    ptslot = PTS[n % 3]
                    pT = smallb[:, ptslot, :]
                    bob = 6 + n % 2
                    for hh in range(2):
                        o_ = psb[bob][:, hh * 128:(hh + 1) * 128]
                        vl = [vtile(q4)[:, hh * 128:(hh + 1) * 128]]
                        rk = [("smbv", q4), ("smb", ptslot, hh)]
                        if bq > 0:
                            vl.append(vtile(pq4)[:, hh * 128:(hh + 1) * 128])
                            rk.append(("smbv", pq4))
                        for kb in range(nblk):
                            P.pe(_I("matmul", o_, vl[kb], pT[:, hh * 256 + kb * 128: hh * 256 + (kb + 1) * 128],
                                    start=(kb == 0), stop=(kb == nblk - 1)), r=rk, w=[PK(bob)])
                    ov = psb[bob][:, 0:256].rearrange("p (a b) -> p a b", a=2)
                    av = acc[:, :, tok]
                    if p == 0:
                        P.act(_I("activation", out=av, in_=ov, func=AF.Identity), r=[PK(bob)], w=ACCK)
                    else:
                        P.dve(_I("tensor_tensor", out=av, in0=ov, in1=av, op=ALU.add), r=[PK(bob)] + ACCK, w=ACCK)

                N_ = len(its)
                for n in range(min(DEPTH_A, N_)):
                    stage_a(n)
                for n in range(N_):
                    if n + DEPTH_A < N_:
                        stage_a(n + DEPTH_A)
                    stage_b(n)
                tmp = S(8)
                for tt in range(NTT):
                    ts = slice(tt * TT, (tt + 1) * TT)
                    P.dve(_I("reciprocal", out=tmp[0:64, :], in_=acc[64:128, 0, ts]), r=ACCK, w=[SK(8)])
                    P.dve(_I("tensor_tensor", out=big[0:64, c, ts], in0=acc[0:64, 0, ts], in1=tmp[0:64, :], op=ALU.mult),
                          r=ACCK + [SK(8)], w=[("big", c, tt)])
                    P.dve(_I("reciprocal", out=tmp[0:64, :], in_=acc[64:128, 1, ts]), r=ACCK + [SK(8)], w=[SK(8)])
                    P.dve(_I("tensor_tensor", out=big[64:128, c, ts], in0=acc[0:64, 1, ts], in1=tmp[0:64, :], op=ALU.mult),
                          r=ACCK + [SK(8)], w=[("big", c, tt)])
            P.dve(_I("memset", dummy[:, 1:2], 0.0), r=SUBK, w=WHOLEK + ["dummy"])

        def stage_mem(l, s):
            memf = scr[:, 0:4, :].rearrange("p a b -> p (a b)").rearrange("p (c m) -> p c m", c=8)
            MK = [SK(i) for i in range(4)]
            MNK = [("smb", i) for i in range(4)]; KNK = [SK(4), SK(5)]; VMK = [SK(6), SK(7)]
            P.dma(_I("dma_start", out=memf, in_=memT[s].rearrange("(c p) m -> p c m", p=128)), "memf", w=MK)
            b = bank()
            for c in range(8):
                sq = sqb[:, c % 2, 0:MEM]
                P.act(_I("activation", out=sq, in_=memf[:, c, :], func=AF.Square), r=MK, w=[("sqb", c % 2)])
                P.pe(_I("matmul", psb[b][:, 0:MEM], ones[:], sq, start=(c == 0), stop=(c == 7)),
                     r=[("sqb", c % 2), "ones"], w=[PK(b)])
            rs = rstd_from_ps(b, MEM, 1.0 / D, 0)
            for c in range(8):
                P.dve(_I("scalar_tensor_tensor", out=memn[:, c, :], in0=memf[:, c, :], scalar=spc(l, "g_memkv", c),
                                                            in1=rs, op0=ALU.mult, op1=ALU.mult),
                      r=MK + [("rsb", 0), "spt"], w=MNK)
            for i in range(4):
                wp, wk = w_next("mk%d" % i)
                bj = []
                for j in range(2):
                    b = bank(); bj.append(b)
                    mm_group(b, [wp[:, kc, j * 128:(j + 1) * 128] for kc in range(8)], [memn[:, kc, :] for kc in range(8)],
                             [wk] + MNK, N=MEM)
                b2 = bank()
                for j in range(2):
                    sq = sqb[:, j, 0:MEM]
                    P.act(_I("activation", out=sq, in_=psb[bj[j]][:, 0:MEM], func=AF.Square),
                          r=[PK(bj[j])], w=[("sqb", j)])
                    P.pe(_I("matmul", psb[b2][:, 0:MEM], ones[:], sq, start=(j == 0), stop=(j == 1)),
                         r=[("sqb", j), "ones"], w=[PK(b2)])
                rs = rstd_from_ps(b2, MEM, 1.0 / 256.0, 1)
                for j in range(2):
                    P.dve(_I("scalar_tensor_tensor",
                        out=kn[:, 2 * i + j, :], in0=psb[bj[j]][:, 0:MEM], scalar=spc(l, "mkg", j), in1=rs, op0=ALU.mult, op1=ALU.mult),
                        r=[PK(bj[j]), ("rsb", 1), "spt"], w=KNK)
            for i in range(4):
                wp, wk = w_next("mv%d" % i)
                for mb in range(2):
                    b = bank()
                    mm_group(b, [memn[:, kc, mb * 128:(mb + 1) * 128] for kc in range(8)], [wp[:, kc, :] for kc in range(8)],
                             [wk] + MNK, N=256)
                    P.act(_I("activation", out=vmem[:, mb, i * 256:(i + 1) * 256], in_=psb[b][:, 0:256],
                                                                  func=AF.Identity), r=[PK(b)], w=VMK)
            sqbufs = [(sqb[:, 0, :], ("sqb", 0)), (sqb[:, 1, :], ("sqb", 1)), (smallb[:, 4, :], ("smb", 4)), (smallb[:, 5, :], ("smb", 5))]
            pendq = []

            def qtail(i, tt, bj, n):
                ts = slice(tt * TT, (tt + 1) * TT)
                b2 = bank()
                for j in range(2):
                    sq, sqk = sqbufs[(n % 2) * 2 + j]
                    P.pe(_I("matmul", psb[b2][:, :], ones[:], sq, start=(j == 0), stop=(j == 1)), r=[sqk, "ones"], w=[PK(b2)])
                slot = n % 2
                rs = rstd_from_ps(b2, TT, 1.0 / 256.0, slot)
                for j in range(2):
                    P.dve(_I("scalar_tensor_tensor", out=big[:, 2 * i + j, ts], in0=psb[bj[j]][:, :], scalar=der[:, 4 + j:5 + j],
                             in1=rs, op0=ALU.mult, op1=ALU.mult), r=[PK(bj[j]), ("rsb", slot), "der"], w=[("big", 2 * i + j, tt)])

            nq = 0
            for i in range(4):
                wp, wk = w_next("mq%d" % i)
                for tt in range(NTT):
                    ts = slice(tt * TT, (tt + 1) * TT)
                    bj = []
                    for j in range(2):
                        b = bank(); bj.append(b)
                        mm_group(b, [wp[:, kc, j * 128:(j + 1) * 128] for kc in range(8)], [hn[:, kc, ts] for kc in range(8)],
                                 [wk] + hn_keys(tt))
                    while pendq:
                        qtail(*pendq.pop(0))
                    for j in range(2):
                        sq, sqk = sqbufs[(nq % 2) * 2 + j]
                        P.act(_I("activation", out=sq, in_=psb[bj[j]][:, :], func=AF.Square), r=[PK(bj[j])], w=[sqk])
                    pendq.append((i, tt, bj, nq))
                    nq += 1
            while pendq:
                qtail(*pendq.pop(0))
            for tt in range(NTT):
                ts = slice(tt * TT, (tt + 1) * TT)
                for i in range(4):
                    pts = []
                    for mb in range(2):
                        b = bank()
                        mm_group(b, [kn[:, 2 * i + j, mb * 128:(mb + 1) * 128] for j in range(2)],
                                 [big[:, 2 * i + j, ts] for j in range(2)], KNK + [("big", 2 * i + j, tt) for j in range(2)])
                        pT = smallb[:, mb + 2 * (i % 2), :]
                        pk = ("smb", mb + 2 * (i % 2))
                        P.act(_I("activation", out=pT, in_=psb[b][:, :], func=AF.Exp), r=[PK(b)], w=[pk])
                        pts.append((pT, pk))
                    bd_ = bank()
                    mm_group(bd_, [ones[:], ones[:]], [pts[0][0], pts[1][0]], ["ones", pts[0][1], pts[1][1]])
                    rd = S(8 + (i % 2))
                    P.act(_I("activation", out=rd, in_=psb[bd_][:, :], func=AF.Identity), r=[PK(bd_)],
                          w=[SK(8 + (i % 2))])
                    P.dve(_I("reciprocal", out=rd, in_=rd), r=[SK(8 + (i % 2))], w=[SK(8 + (i % 2))])
                    for j in range(2):
                        bn = bank()
                        mm_group(bn, [vmem[:, mb, i * 256 + j * 128: i * 256 + (j + 1) * 128] for mb in range(2)],
                                 [pts[0][0], pts[1][0]], VMK + [pts[0][1], pts[1][1]])
                        P.dve(_I("tensor_tensor", out=hn[:, 2 * i + j, ts], in0=psb[bn][:, :],
                                                                                      in1=rd, op=ALU.mult),
                              r=[PK(bn), SK(8 + (i % 2))], w=[("hn", 2 * i + j, tt)])
            for i in range(4):
                wp, wk = w_next("mo%d" % i)
                for tt in range(NTT):
                    ts = slice(tt * TT, (tt + 1) * TT)
                    for m in range(2):
                        b = bank()
                        mm_group(b, [wp[:, kc, m * 128:(m + 1) * 128] for kc in range(8)], [hn[:, kc, ts] for kc in range(8)],
                                 [wk] + hn_keys(tt))
                        resid_add(b, 2 * i + m, tt)

        def stage_ffn(l):
            for (a, bnd) in FFN_PARTS:
                for j in range(a, bnd):
                    jj = j - a
                    wp, wk = w_next("up%d" % j)
                    gbuf = scr[:, 0:2, :].rearrange("p a b -> p (a b)")
                    P.dve(_I("memset", gbuf[:, 0:2], 0.0), w=[SK(0)])
                    for tt in range(NTT):
                        ts = slice(tt * TT, (tt + 1) * TT)
                        bg = bank(); bu = bank()
                        mm_group(bg, [wp[:, kc, 0:128] for kc in range(8)], [hn[:, kc, ts] for kc in range(8)], [wk] + hn_keys(tt))
                        mm_group(bu, [wp[:, kc, 128:256] for kc in range(8)], [hn[:, kc, ts] for kc in range(8)], [wk] + hn_keys(tt))
                        P.act(_I("activation", out=gbuf[:, 2:2 + TT], in_=psb[bg][:, :], func=AF.Identity),
                              r=[PK(bg)], w=[SK(0), SK(1)])
                        n_it = st.setdefault("fi", 0); st["fi"] = n_it + 1
                        tslot = 2 + (n_it % 2)
                        t1 = S(tslot)
                        P.dve(_I("tensor_scalar", out=t1, in0=gbuf[:, 2:2 + TT], scalar1=spc(l, "fcw", 0 * 22 + j),
                                                               scalar2=spc(l, "fcb", j), op0=ALU.mult, op1=ALU.add),
                              r=[SK(0), SK(1), "spt"], w=[SK(tslot)])
                        for k in range(1, 3):
                            P.dve(_I("scalar_tensor_tensor", out=t1, in0=gbuf[:, 2 - k:2 - k + TT],
                                                                               scalar=spc(l, "fcw", k * 22 + j), in1=t1,
                                                                               op0=ALU.mult, op1=ALU.add),
                                  r=[SK(0), SK(1), SK(tslot), "spt"], w=[SK(tslot)])
                        P.dve(_I("tensor_copy", out=gbuf[:, 0:2], in_=gbuf[:, TT:TT + 2]), r=[SK(0), SK(1)], w=[SK(0)])
                        P.act(_I("activation", out=t1, in_=t1, func=AF.Gelu_apprx_tanh), r=[SK(tslot)], w=[SK(tslot)])
                        P.dve(_I("tensor_tensor", out=big[:, jj, ts], in0=psb[bu][:, :], in1=t1,
                                                                                    op=ALU.mult),
                              r=[PK(bu), SK(tslot)], w=[("big", jj, tt)])
                K = bnd - a
                for i in range(4):
                    wp, wk = w_next("dn%d_%d" % (a, i))
                    for tt in range(NTT):
                        ts = slice(tt * TT, (tt + 1) * TT)
                        for m in range(2):
                            b = bank()
                            mm_group(b, [wp[:, k, m * 128:(m + 1) * 128] for k in range(K)], [big[:, k, ts] for k in range(K)],
                                     [wk] + [("big", k, tt) for k in range(K)])
                            resid_add(b, 2 * i + m, tt)

        w_issue_upto(LOOKAHEAD)
        stages = ["norm1", "lru", "pool", "woA", "qkv", "attn", "woB", "norm2", "mem", "norm3", "ffn"]
        n_stage = len(stages) if debug_stage is None else stages.index(debug_stage) + 1
        for s in range(nseq):
            for c in range(8):
                P.dma(_I("dma_start", out=h[:, c, :], in_=xT[s, c * 128:(c + 1) * 128, :]), "hx",
                      w=[("h", c, tt) for tt in range(NTT)])
            for l in range(nlayers):
                par = 0
                layer_setup(l, par)
                fns = [lambda: rmsnorm_h(l, "g_mix"), lambda: stage_lru(l, par), lambda: stage_pool(l, par),
                       lambda: stage_wout("woA", [0, 1, 2, 3, 4]), lambda: stage_qkv(l), lambda: stage_attn(l),
                       lambda: stage_wout("woB", [0, 1, 2]), lambda: rmsnorm_h(l, "g_mem"), lambda: stage_mem(l, s),
                       lambda: rmsnorm_h(l, "g_ffn"), lambda: stage_ffn(l)]
                for f in fns[:n_stage]:
                    f()
            for c in range(8):
                P.dma(_I("dma_start", out=outT[s, c * 128:(c + 1) * 128, :], in_=h[:, c, :]), "ho",
                      r=[("h", c, tt) for tt in range(NTT)], w=["outT"])
        if debug_stage is not None:
            dbg_big = nc.dram_tensor("dbg_big", [128, 9 * SEQ], F32, kind="ExternalOutput").ap()
            dbg_hn = nc.dram_tensor("dbg_hn", [128, 8 * SEQ], F32, kind="ExternalOutput").ap()
            P.dma(_I("dma_start", out=dbg_big, in_=big[:].rearrange("p a b -> p (a b)")), "dbg",
                  r=[("big", c, tt) for c in range(9) for tt in range(NTT)], w=["dbgk"], q="pool")
            P.dma(_I("dma_start", out=dbg_hn, in_=hn[:].rearrange("p a b -> p (a b)")), "dbg",
                  r=[("hn", c, tt) for c in range(8) for tt in range(NTT)], w=["dbgk"], q="pool")
        P.op("sp", lambda e: None, reads=["outT", "dbgk"])
        P.emit()
    return nc


_CACHE = {}


def host_prepare(inp, nlayers=DEPTH):
    pieces = [weight_pieces_layer(inp, l) for l in range(DEPTH)]
    meta = [(nm, k, n) for nm, k, n, _ in pieces[0]]
    wst = np.concatenate([a for l in range(DEPTH) for _, _, _, a in pieces[l]], axis=1)
    wst = np.ascontiguousarray(wst, dtype=np.float32)
    E, cst = const_tables()
    shared = {"wst": wst, "spd": small_params(inp), "bdd": blockdiag_params(inp), "Ed": E, "cstd": cst}
    return meta, shared


def kernel(**inputs):
    inp = {k: np.asarray(v) for k, v in inputs.items()}
    x = inp["x"]; mem = inp["mem"]
    B = x.shape[0]
    nseq = B // NCORE
    meta, shared = host_prepare(inp)
    key = (nseq, DEPTH)
    nc = build_program(nseq=nseq, nlayers=DEPTH, piece_meta=meta)
    xT = np.ascontiguousarray(x.transpose(0, 2, 1))
    memT = np.ascontiguousarray(mem.transpose(0, 2, 1))
    in_maps = []
    for c in range(NCORE):
        m = dict(shared)
        m["xT"] = xT[c * nseq:(c + 1) * nseq]
        m["memT"] = memT[c * nseq:(c + 1) * nseq]
        in_maps.append(m)
    res = run_bass_kernel_spmd(nc, in_maps, core_ids=list(range(NCORE)))
    outT = np.concatenate([r["outT"] for r in res.results], axis=0)
    return np.ascontiguousarray(outT.transpose(0, 2, 1)).astype(np.float32)
```

```python
import bisect
from contextlib import ExitStack

import numpy as np
import concourse.bass as bass
import concourse.mybir as mybir
from concourse.bass_utils import run_bass_kernel_spmd

F32 = mybir.dt.float32
BF16 = mybir.dt.bfloat16
AF = mybir.ActivationFunctionType
ALU = mybir.AluOpType

D = 1024
SEQ = 2048
DEPTH = 4
NCORE = 8
MEM = 256
DFF = 2816
INW = 2176
EPS = 1e-6
TT = 512
NTT = SEQ // TT
PATTERNS = ((128, 1), (512, 4), (2048, 16))
NSLOT = 5
SLOT_E = 2048
LOOKAHEAD = 3
FFN_PARTS = ((0, 8), (8, 15), (15, 22))
SEM_LIMIT = 30000


def _I(m, *a, **k):
    return (m, a, k)


class _Op:
    __slots__ = ("eng", "fn", "deps", "sig", "semi", "val", "dma", "dsem", "dval", "idx")


class Prog:
    ENGS = ("pe", "act", "dve", "pool", "sp")

    def __init__(self, nc, es):
        self.nc = nc
        self.es = es
        self.ops = []
        self.lastw = {}
        self.readers = {}
        self.dma_sems = {}
        self.n_sems = 0

    def _new_sem(self, name):
        self.n_sems += 1
        return self.es.enter_context(self.nc.semaphore(name))

    def op(self, eng, fn, reads=(), writes=(), dma=None):
        o = _Op()
        o.eng = eng; o.fn = fn; o.sig = False; o.dma = dma; o.idx = len(self.ops)
        deps = set()
        lastw = self.lastw; readers = self.readers
        for k in reads:
            w = lastw.get(k)
            if w is not None:
                deps.add(w)
        for k in writes:
            w = lastw.get(k)
            if w is not None:
                deps.add(w)
            rl = readers.get(k)
            if rl:
                deps.update(rl)
        if eng == "pe" and dma is None:
            deps = {d for d in deps if not (d.eng == "pe" and d.dma is None)}
        o.deps = deps
        for d in deps:
            d.sig = True
        for k in writes:
            lastw[k] = o
            readers[k] = []
        for k in reads:
            rl = readers.get(k)
            if rl is None:
                readers[k] = [o]
            else:
                rl.append(o)
        self.ops.append(o)
        return o

    def pe(self, fn, r=(), w=()): return self.op("pe", fn, r, w)
    def act(self, fn, r=(), w=()): return self.op("act", fn, r, w)
    def dve(self, fn, r=(), w=()): return self.op("dve", fn, r, w)
    def pool(self, fn, r=(), w=()): return self.op("pool", fn, r, w)
    def dma(self, fn, sem, r=(), w=(), q="sp"): return self.op(q, fn, r, w, dma=sem)

    def emit(self):
        nc = self.nc
        cur = {}
        for o in self.ops:
            if o.dma is not None:
                ent = self.dma_sems.get(o.dma)
                if ent is None:
                    ent = [self._new_sem("d_" + o.dma), 0]
                    self.dma_sems[o.dma] = ent
                ent[1] += 16
                o.dsem = ent[0]; o.dval = ent[1]
                continue
            if not o.sig:
                continue
            ent = cur.get(o.eng)
            if ent is None or ent[1] >= SEM_LIMIT:
                ent = [self._new_sem("e_%s_%d" % (o.eng, self.n_sems)), 0]
                cur[o.eng] = ent
            ent[1] += 1
            o.semi = ent[0]; o.val = ent[1]
        dma_hist = {}
        for o in self.ops:
            if o.dma is not None:
                dma_hist.setdefault(o.dma, []).append((o.idx, o.dval))
        dma_idx = {k: [a for a, _ in v] for k, v in dma_hist.items()}
        per_eng = {e: [] for e in self.ENGS}
        for o in self.ops:
            per_eng[o.eng].append(o)

        def run(engname, e):
            waited = {}
            for o in per_eng[engname]:
                need = {}
                for d in o.deps:
                    if d.dma is not None:
                        j = bisect.bisect_left(dma_idx[d.dma], o.idx) - 1
                        sem, val = d.dsem, dma_hist[d.dma][j][1]
                    else:
                        sem, val = d.semi, d.val
                    key = id(sem)
                    if key not in need or need[key][1] < val:
                        need[key] = (sem, val)
                for key, (sem, val) in need.items():
                    if waited.get(key, 0) < val:
                        e.wait_ge(sem, val)
                        waited[key] = val
                f = o.fn
                ins = getattr(e, f[0])(*f[1], **f[2]) if isinstance(f, tuple) else f(e)
                if ins is None:
                    continue
                if o.dma is not None:
                    ins.then_inc(o.dsem, 16)
                elif o.sig:
                    ins.then_inc(o.semi, 1)

        with nc.Block() as block:
            @block.tensor
            def _(e): run("pe", e)

            @block.scalar
            def _(e): run("act", e)

            @block.vector
            def _(e): run("dve", e)

            @block.gpsimd
            def _(e): run("pool", e)

            @block.sync
            def _(e): run("sp", e)


def _piece(W, rows, cols):
    blk = np.stack([np.concatenate([W[r:r + 128, c:c + 128] for c in cols], axis=1) for r in rows], axis=1)
    return blk.reshape(128, -1)


def weight_pieces_layer(inp, l):
    w_in = inp["w_in"][l]; w_out = inp["w_out"][l]; w_q = inp["w_q_mem"][l]; w_kv = inp["w_kv_mem"][l]
    w_o = inp["w_o_mem"][l]; w_up = inp["w_up"][l]; w_down = inp["w_down"][l]
    k8 = [i * 128 for i in range(8)]
    out = []
    for j in range(3):
        out.append(("lru%d" % j, 8, 256, _piece(w_in, k8, [1408 + j * 128, 1792 + j * 128])))
    out.append(("pool", 8, 256, _piece(w_in, k8, [0, 128])))
    rowsA = [640, 768, 896, 0, 128]
    for i in range(4):
        out.append(("woA%d" % i, 5, 256, _piece(w_out, rowsA, [i * 256, i * 256 + 128])))
    qkv_cols = [256 + i * 128 for i in range(9)]
    for i in range(0, 9, 2):
        cs = qkv_cols[i:i + 2]
        out.append(("qkv%d" % (i // 2), 8, 128 * len(cs), _piece(w_in, k8, cs)))
    rowsB = [256, 384, 512]
    for i in range(4):
        out.append(("woB%d" % i, 3, 256, _piece(w_out, rowsB, [i * 256, i * 256 + 128])))
    for i in range(4):
        out.append(("mk%d" % i, 8, 256, _piece(w_kv, k8, [i * 256, i * 256 + 128])))
    for i in range(4):
        out.append(("mv%d" % i, 8, 256, _piece(w_kv, k8, [1024 + i * 256, 1024 + i * 256 + 128])))
    for i in range(4):
        out.append(("mq%d" % i, 8, 256, _piece(w_q, k8, [i * 256, i * 256 + 128])))
    for i in range(4):
        out.append(("mo%d" % i, 8, 256, _piece(w_o, k8, [i * 256, i * 256 + 128])))
    for (a, b) in FFN_PARTS:
        for j in range(a, b):
            out.append(("up%d" % j, 8, 256, _piece(w_up, k8, [j * 128, DFF + j * 128])))
        rows = [j * 128 for j in range(a, b)]
        for i in range(4):
            out.append(("dn%d_%d" % (a, i), b - a, 256, _piece(w_down, rows, [i * 256, i * 256 + 128])))
    return out


class SP:
    names = [("g_mix", 8), ("g_mem", 8), ("g_memkv", 8), ("g_ffn", 8), ("pool_scale", 2), ("q_gain", 1), ("k_gain", 1),
             ("lcw", 12), ("lcb", 3), ("lba", 3), ("lbx", 3), ("llam", 3), ("mqg", 2), ("mkg", 2),
             ("fcw", 66), ("fcb", 22)]
    off = {}
    n = 0
    for _nm, _w in names:
        off[_nm] = n
        n += _w
    PER_LAYER = n


def fm(v, nchunk):
    return np.ascontiguousarray(v.reshape(nchunk, 128).T)


def small_params(inp):
    t = np.zeros((128, DEPTH, SP.PER_LAYER), np.float32)
    for l in range(DEPTH):
        def put(name, arr):
            o = SP.off[name]
            t[:, l, o:o + arr.shape[1]] = arr
        put("g_mix", fm(inp["norm_mix"][l], 8)); put("g_mem", fm(inp["norm_mem"][l], 8))
        put("g_memkv", fm(inp["norm_memkv"][l], 8)); put("g_ffn", fm(inp["norm_ffn"][l], 8))
        put("pool_scale", fm(inp["pool_scale"][l], 2))
        put("q_gain", np.tile(inp["q_gain"][l], 2)[:, None]); put("k_gain", np.tile(inp["k_gain"][l], 2)[:, None])
        put("lcw", np.concatenate([fm(inp["lru_conv_w"][l, k], 3) for k in range(4)], axis=1))
        put("lcb", fm(inp["lru_conv_b"][l], 3)); put("lba", fm(inp["lru_ba"][l], 3)); put("lbx", fm(inp["lru_bx"][l], 3))
        put("llam", fm(inp["lru_lambda"][l], 3))
        put("mqg", fm(inp["mq_gain"][l], 2)); put("mkg", fm(inp["mk_gain"][l], 2))
        put("fcw", np.concatenate([fm(inp["ffn_conv_w"][l, k], 22) for k in range(3)], axis=1))
        put("fcb", fm(inp["ffn_conv_b"][l], 22))
    return t.reshape(128, DEPTH * SP.PER_LAYER)


def blockdiag_params(inp):
    t = np.zeros((128, DEPTH, 8, 128), np.float32)
    for l in range(DEPTH):
        for c in range(2):
            for hh in range(2):
                t[hh * 64:(hh + 1) * 64, l, c, hh * 64:(hh + 1) * 64] = inp["pool_w"][l, 2 * c + hh]
        for c in range(3):
            for hh in range(2):
                t[hh * 64:(hh + 1) * 64, l, 2 + c, hh * 64:(hh + 1) * 64] = inp["lru_wa"][l, 2 * c + hh]
                t[hh * 64:(hh + 1) * 64, l, 5 + c, hh * 64:(hh + 1) * 64] = inp["lru_wx"][l, 2 * c + hh]
    return t.reshape(128, DEPTH * 8 * 128)


def const_tables():
    slopes = np.array([2.0 ** (-8.0 * (h + 1) / 6) for h in range(6)], np.float64)
    j = np.arange(128)[:, None].astype(np.float64); i = np.arange(128)[None, :].astype(np.float64)
    E = np.zeros((128, 18, 2, 128), np.float64)
    for p, (w, d) in enumerate(PATTERNS):
        for h in range(6):
            S = slopes[h] * d
            dist_c = i - j
            E[:, p * 6 + h, 0, :] = np.where(dist_c >= 0, np.exp(-S * np.maximum(dist_c, 0)), 0.0)
            dist_p = 128 + i - j
            E[:, p * 6 + h, 1, :] = np.where(dist_p <= 128, np.exp(-S * dist_p), 0.0)
    invc = np.zeros((128, 2, 16), np.float64); invw = np.zeros((128, 2), np.float64)
    wins = ((2, 4), (8, 16))
    t = np.arange(16)
    for c in range(2):
        for hh in range(2):
            w = wins[c][hh]
            invc[hh * 64:(hh + 1) * 64, c, :] = 1.0 / np.minimum(t + 1, w)
            invw[hh * 64:(hh + 1) * 64, c] = 1.0 / w
    cst = np.concatenate([invc.reshape(128, 32), invw], axis=1).astype(np.float32)
    return E.reshape(128, 18 * 256).astype(np.float32), cst


def build_program(nseq=4, nlayers=DEPTH, piece_meta=None, debug_stage=None):
    nc = bass.Bass("TRN2", target_bir_lowering=False)
    per_layer_elems = sum(k * n for _, k, n in piece_meta)
    xT = nc.dram_tensor("xT", [nseq, D, SEQ], F32, kind="ExternalInput").ap()
    memT = nc.dram_tensor("memT", [nseq, D, MEM], F32, kind="ExternalInput").ap()
    wst = nc.dram_tensor("wst", [128, nlayers * per_layer_elems], F32, kind="ExternalInput").ap()
    spd = nc.dram_tensor("spd", [128, DEPTH * SP.PER_LAYER], F32, kind="ExternalInput").ap()
    bdd = nc.dram_tensor("bdd", [128, DEPTH * 8 * 128], F32, kind="ExternalInput").ap()
    Ed = nc.dram_tensor("Ed", [128, 18 * 256], F32, kind="ExternalInput").ap()
    cstd = nc.dram_tensor("cstd", [128, 34], F32, kind="ExternalInput").ap()
    outT = nc.dram_tensor("outT", [nseq, D, SEQ], F32, kind="ExternalOutput").ap()

    es = ExitStack()
    with es:
        P = Prog(nc, es)
        sb = lambda name, shape, dt: es.enter_context(nc.sbuf_tensor(name, shape, dt))
        h = sb("h", [128, 8, SEQ], F32)
        hn = sb("hn", [128, 8, SEQ], BF16)
        big = sb("big", [128, 9, SEQ], BF16)
        wsl = sb("wsl", [128, NSLOT, SLOT_E], BF16)
        scr = sb("scr", [128, 12, TT], F32)
        Eb = sb("Eb", [128, 18, 256], BF16)
        spt = sb("spt", [128, DEPTH, SP.PER_LAYER], F32)
        bdt = sb("bdt", [128, 1, 8, 128], BF16)
        cst = sb("cst", [128, 34], F32)
        dummy = sb("dummyt", [128, 2], F32)
        der = sb("der", [128, 8], F32)
        ones = sb("ones", [128, 128], BF16)
        bones = sb("bones", [128, 128], BF16)
        ident = sb("ident", [128, 128], BF16)
        sqb = sb("sqb", [128, 2, TT], BF16)
        rsb = sb("rsb", [128, 2, TT], F32)
        smallb = sb("smallb", [128, 6, TT], BF16)
        idf = scr[:, 11, 0:128]
        memn = smallb[:, 0:4, :].rearrange("p a b -> p (a b)").rearrange("p (c m) -> p c m", c=8)
        kn = scr[:, 4:6, :].rearrange("p a b -> p (a b)").bitcast(BF16).rearrange("p (c m) -> p c m", c=8)
        vmem = scr[:, 6:8, :].rearrange("p a b -> p (a b)").bitcast(BF16).rearrange("p (c m) -> p c m", c=2)
        psb = [es.enter_context(nc.psum_tensor("ps%d" % i, [128, TT], F32)) for i in range(8)]

        st = {"bank": 0}

        def bank():
            b = st["bank"]
            st["bank"] = (b + 1) % 8
            return b

        def PK(b): return ("ps", b)

        stream = []
        for s in range(nseq):
            for l in range(nlayers):
                off = l * per_layer_elems
                for i, (nm, kc, n) in enumerate(piece_meta):
                    stream.append((nm, off, kc, n))
                    off += kc * n
        wstate = {"issued": 0, "next": 0}

        def w_issue_upto(k):
            while wstate["issued"] < min(k, len(stream)):
                i = wstate["issued"]
                nm, off, kc, n = stream[i]
                slot = i % NSLOT
                src = wst[:, off:off + kc * n]
                dst = wsl[:, slot, 0:kc * n]
                P.dma(_I("dma_start", out=dst, in_=src), "w%d" % slot, w=[("w", slot)], q="pool")
                wstate["issued"] += 1

        def w_next(expect):
            i = wstate["next"]
            nm, off, kc, n = stream[i]
            assert nm == expect, (nm, expect)
            w_issue_upto(i + 1 + LOOKAHEAD)
            wstate["next"] += 1
            slot = i % NSLOT
            return wsl[:, slot, 0:kc * n].rearrange("p (k n) -> p k n", k=kc), ("w", slot)

        P.dma(_I("dma_start", out=spt[:].rearrange("p a b -> p (a b)"), in_=spd), "c0", w=["spt"])
        P.dma(_I("dma_start", out=cst[:], in_=cstd), "c1", w=["cst"])
        P.dma(_I("dma_start", out=Eb[:].rearrange("p a b -> p (a b)"), in_=Ed), "c2", w=["Eb"], q="pool")
        P.dve(_I("memset", ones[:], 1.0), w=["ones"])
        P.dve(_I("memset", bones[:], 0.0), w=["bones"])
        P.dve(_I("memset", bones[0:64, 0:64], 1.0), w=["bones"])
        P.dve(_I("memset", bones[64:128, 64:128], 1.0), w=["bones"])
        P.pool(_I("memset", idf[:], 1.0), w=["idf"])
        P.pool(_I("affine_select", out=idf[:], in_=idf[:], pattern=[[-1, 128]], compare_op=ALU.is_equal,
                                         fill=0.0, base=0, channel_multiplier=1), r=["idf"], w=["idf"])
        P.dve(_I("tensor_copy", out=ident[:], in_=idf[:]), r=["idf"], w=["ident"])

        def spc(l, name, j=0, w=1):
            o = SP.off[name] + j
            return spt[:, l, o:o + w]

        def mm_group(b, lhs, rhs, rkeys, N=TT, M=128, po=0, co=0, tp=None):
            ps = psb[b][po:po + M, co:co + N]
            n = len(lhs)
            for i in range(n):
                if tp is None:
                    P.pe(_I("matmul", ps, lhs[i], rhs[i], start=(i == 0), stop=(i == n - 1)), r=rkeys, w=[PK(b)])
                else:
                    P.pe(_I("matmul", ps, lhs[i], rhs[i], start=(i == 0), stop=(i == n - 1), tile_position=tp), r=rkeys, w=[PK(b)])

        def rstd_from_ps(b, N, inv_n, slot):
            rs = rsb[:, slot, 0:N]
            P.act(_I("activation", out=rs, in_=psb[b][:, 0:N], func=AF.Sqrt, bias=EPS, scale=inv_n),
                  r=[PK(b)], w=[("rsb", slot)])
            P.dve(_I("reciprocal", out=rs, in_=rs), r=[("rsb", slot)], w=[("rsb", slot)])
            return rs

        def rmsnorm_h(l, gname):
            for tt in range(NTT):
                ts = slice(tt * TT, (tt + 1) * TT)
                b = bank()
                for c in range(8):
                    sq = sqb[:, c % 2, :]
                    P.act(_I("activation", out=sq, in_=h[:, c, ts], func=AF.Square),
                          r=[("h", c, tt)], w=[("sqb", c % 2)])
                    P.pe(_I("matmul", psb[b][:, :], ones[:], sq, start=(c == 0), stop=(c == 7)),
                         r=[("sqb", c % 2), "ones"], w=[PK(b)])
                slot = tt % 2
                rs = rstd_from_ps(b, TT, 1.0 / D, slot)
                for c in range(8):
                    P.dve(_I("scalar_tensor_tensor", out=hn[:, c, ts], in0=h[:, c, ts], scalar=spc(l, gname, c),
                                                                      in1=rs, op0=ALU.mult, op1=ALU.mult),
                          r=[("h", c, tt), ("rsb", slot), "spt"], w=[("hn", c, tt)])

        def hn_keys(tt): return [("hn", c, tt) for c in range(8)]

        def resid_add(b, oc, tt):
            ts = slice(tt * TT, (tt + 1) * TT)
            P.dve(_I("tensor_tensor", out=h[:, oc, ts], in0=psb[b][:, :], in1=h[:, oc, ts], op=ALU.add),
                  r=[PK(b), ("h", oc, tt)], w=[("h", oc, tt)])

        def S(i): return scr[:, i, :]

        def SK(i): return ("scr", i)

        def layer_setup(l, par):
            src = bdd[:, l * 1024:(l + 1) * 1024]
            P.dma(_I("dma_start", out=bdt[:, par, :, :].rearrange("p a b -> p (a b)"), in_=src), "bd%d" % par,
                  w=[("bdt", par)], q="pool")
            P.act(_I("activation", out=der[:, 0:3], in_=spc(l, "llam", 0, 3), func=AF.Exp, scale=-1.0),
                  r=["spt"], w=["der"])
            P.act(_I("activation", out=der[:, 0:3], in_=der[:, 0:3], func=AF.Ln, bias=1.0), r=["der"], w=["der"])
            P.dve(_I("tensor_scalar", out=der[:, 0:3], in0=der[:, 0:3], scalar1=-8.0, scalar2=None, op0=ALU.mult),
                  r=["der"], w=["der"])
            P.dve(_I("tensor_scalar", out=der[:, 3:4], in0=spc(l, "q_gain"), scalar1=0.125, scalar2=None, op0=ALU.mult),
                  r=["spt", "der"], w=["der"])
            P.dve(_I("tensor_scalar", out=der[:, 4:6], in0=spc(l, "mqg", 0, 2), scalar1=1.0 / 16.0, scalar2=None,
                                            op0=ALU.mult), r=["spt", "der"], w=["der"])

        def stage_lru(l, par):
            xbuf = scr[:, 0:2, :].rearrange("p a b -> p (a b)")
            units = [(j, tt) for j in range(3) for tt in range(NTT)]
            wps = {}

            def a_unit(n):
                j, tt = units[n]
                if tt == 0:
                    wps[j] = w_next("lru%d" % j)
                    P.dve(_I("memset", xbuf[:, 0:3], 0.0), w=[SK(0)])
                wp, wk = wps[j]
                ts = slice(tt * TT, (tt + 1) * TT)
                gs = 2 if n % 2 == 0 else 10
                xs = 3 if n % 2 == 0 else 11
                bx = bank(); by = bank()
                mm_group(bx, [wp[:, kc, 0:128] for kc in range(8)], [hn[:, kc, ts] for kc in range(8)], [wk] + hn_keys(tt))
                mm_group(by, [wp[:, kc, 128:256] for kc in range(8)], [hn[:, kc, ts] for kc in range(8)], [wk] + hn_keys(tt))
                P.act(_I("activation", out=xbuf[:, 3:3 + TT], in_=psb[bx][:, :], func=AF.Identity), r=[PK(bx)], w=[SK(0), SK(1)])
                P.act(_I("activation", out=S(gs), in_=psb[by][:, :], func=AF.Gelu_apprx_tanh), r=[PK(by)], w=[SK(gs)])
                xc = S(xs)
                P.dve(_I("tensor_scalar", out=xc, in0=xbuf[:, 3:3 + TT], scalar1=spc(l, "lcw", 0 * 3 + j),
                         scalar2=spc(l, "lcb", j), op0=ALU.mult, op1=ALU.add), r=[SK(0), SK(1), "spt"], w=[SK(xs)])
                for k in range(1, 4):
                    P.dve(_I("scalar_tensor_tensor", out=xc, in0=xbuf[:, 3 - k:3 - k + TT], scalar=spc(l, "lcw", k * 3 + j),
                             in1=xc, op0=ALU.mult, op1=ALU.add), r=[SK(0), SK(1), SK(xs), "spt"], w=[SK(xs)])
                P.dve(_I("tensor_copy", out=xbuf[:, 0:3], in_=xbuf[:, TT:TT + 3]), r=[SK(0), SK(1)], w=[SK(0)])
                P.act(_I("activation", out=smallb[:, n % 2, :], in_=xc, func=AF.Identity), r=[SK(xs)], w=[("smb", n % 2)])

            def b_unit(n):
                j, tt = units[n]
                ts = slice(tt * TT, (tt + 1) * TT)
                gs = 2 if n % 2 == 0 else 10
                xs = 3 if n % 2 == 0 else 11
                xc = S(xs); gy = S(gs); xcb = smallb[:, n % 2, :]
                ba_ = bank(); bxg = bank()
                mm_group(ba_, [bdt[:, par, 2 + j, :]], [xcb], [("bdt", par), ("smb", n % 2)])
                mm_group(bxg, [bdt[:, par, 5 + j, :]], [xcb], [("bdt", par), ("smb", n % 2)])
                rg = S(4); ig = S(5); aa = S(6); t1 = S(7)
                P.act(_I("activation", out=rg, in_=psb[ba_][:, :], func=AF.Sigmoid, bias=spc(l, "lba", j)), r=[PK(ba_), "spt"], w=[SK(4)])
                P.act(_I("activation", out=ig, in_=psb[bxg][:, :], func=AF.Sigmoid, bias=spc(l, "lbx", j)), r=[PK(bxg), "spt"], w=[SK(5)])
                P.act(_I("activation", out=aa, in_=rg, func=AF.Exp, scale=der[:, j:j + 1]), r=[SK(4), "der"], w=[SK(6)])
                P.act(_I("activation", out=t1, in_=aa, func=AF.Square), r=[SK(6)], w=[SK(7)])
                P.act(_I("activation", out=t1, in_=t1, func=AF.Sqrt, bias=1.0, scale=-1.0), r=[SK(7)], w=[SK(7)])
                P.dve(_I("tensor_tensor", out=ig, in0=ig, in1=xc, op=ALU.mult), r=[SK(5), SK(xs)], w=[SK(5)])
                P.dve(_I("tensor_tensor", out=ig, in0=ig, in1=t1, op=ALU.mult), r=[SK(5), SK(7)], w=[SK(5)])
                hr = S(8 + (tt % 2))
                if tt == 0:
                    P.dve(_I("tensor_tensor_scan", out=hr, data0=aa, data1=ig, initial=0.0, op0=ALU.mult, op1=ALU.add),
                          r=[SK(6), SK(5)], w=[SK(8 + (tt % 2))])
                else:
                    prev = S(8 + ((tt - 1) % 2))
                    P.dve(_I("tensor_tensor_scan", out=hr, data0=aa, data1=ig, initial=prev[:, TT - 1:TT], op0=ALU.mult,
                             op1=ALU.add), r=[SK(6), SK(5), SK(8 + ((tt - 1) % 2))], w=[SK(8 + (tt % 2))])
                P.dve(_I("tensor_tensor", out=big[:, j, ts], in0=hr, in1=gy, op=ALU.mult),
                      r=[SK(8 + (tt % 2)), SK(gs)], w=[("big", j, tt)])

            N_ = len(units)
            a_unit(0)
            for n in range(N_):
                if n + 1 < N_:
                    a_unit(n + 1)
                b_unit(n)

        def stage_pool(l, par):
            wp, wk = w_next("pool")
            u = scr[:, 0:4, :].rearrange("p a b -> p (a b)")
            sA = scr[:, 4:8, :].rearrange("p a b -> p (a b)")
            sB = scr[:, 8:12, :].rearrange("p a b -> p (a b)")
            UK = [SK(i) for i in range(0, 4)]; AK = [SK(i) for i in range(4, 8)]; BK = [SK(i) for i in range(8, 12)]

            def shadd(dst, src, sh, lo, hi, dk, sk):
                P.dve(_I("tensor_copy", out=dst[lo:hi, 0:sh], in_=src[lo:hi, 0:sh]), r=sk, w=dk)
                P.dve(_I("tensor_tensor", out=dst[lo:hi, sh:SEQ], in0=src[lo:hi, sh:SEQ], in1=src[lo:hi, 0:SEQ - sh],
                                                op=ALU.add), r=sk, w=dk)

            for c in range(2):
                for tt in range(NTT):
                    ts = slice(tt * TT, (tt + 1) * TT)
                    b = bank()
                    mm_group(b, [wp[:, kc, c * 128:(c + 1) * 128] for kc in range(8)], [hn[:, kc, ts] for kc in range(8)],
                             [wk] + hn_keys(tt))
                    P.act(_I("activation", out=u[:, ts], in_=psb[b][:, :], func=AF.Identity),
                          r=[PK(b)], w=[SK(tt)])
                shadd(sA, u, 1, 0, 128, AK, UK)
                if c == 0:
                    shadd(sB, sA, 2, 64, 128, BK, AK)
                else:
                    shadd(sB, sA, 2, 0, 128, BK, AK)
                    shadd(sA, sB, 4, 0, 128, AK, BK)
                    shadd(sB, sA, 8, 64, 128, BK, AK)
                pl = big[:, 3 + c, :]
                plk = [("big", 3 + c, tt) for tt in range(NTT)]
                for (lo, hi, F, fk) in ((0, 64, sA, AK), (64, 128, sB, BK)):
                    P.dve(_I("scalar_tensor_tensor",
                        out=pl[lo:hi, 16:SEQ], in0=F[lo:hi, 16:SEQ], scalar=cst[lo:hi, 32 + c:33 + c], in1=u[lo:hi, 16:SEQ],
                        op0=ALU.mult, op1=ALU.subtract), r=fk + UK + ["cst"], w=plk)
                    P.dve(_I("tensor_tensor", out=F[lo:hi, 0:16], in0=F[lo:hi, 0:16],
                                                                        in1=cst[lo:hi, c * 16:(c + 1) * 16], op=ALU.mult),
                          r=fk + ["cst"], w=fk)
                    P.dve(_I("tensor_tensor", out=pl[lo:hi, 0:16], in0=F[lo:hi, 0:16],
                                                                        in1=u[lo:hi, 0:16], op=ALU.subtract),
                          r=fk + UK, w=plk)
                for tt in range(NTT):
                    ts = slice(tt * TT, (tt + 1) * TT)
                    b = bank()
                    mm_group(b, [bdt[:, par, c, :]], [big[:, 3 + c, ts]], [("bdt", par), ("big", 3 + c, tt)])
                    P.act(_I("activation", out=big[:, 3 + c, ts], in_=psb[b][:, :], func=AF.Identity,
                                                             scale=spc(l, "pool_scale", c)),
                          r=[PK(b), "spt"], w=[("big", 3 + c, tt)])

        def stage_wout(name, srcs):
            K = len(srcs)
            for i in range(4):
                wp, wk = w_next("%s%d" % (name, i))
                for tt in range(NTT):
                    ts = slice(tt * TT, (tt + 1) * TT)
                    for m in range(2):
                        b = bank()
                        mm_group(b, [wp[:, k, m * 128:(m + 1) * 128] for k in range(K)], [big[:, srcs[k], ts] for k in range(K)],
                                 [wk] + [("big", srcs[k], tt) for k in range(K)])
                        resid_add(b, 2 * i + m, tt)

        def stage_qkv(l):
            pend = []

            def tail(idx, tt, k2):
                ts = slice(tt * TT, (tt + 1) * TT)
                sq = sqb[:, k2 % 2, :]
                qf = S(k2 % 4)
                b2 = bank()
                mm_group(b2, [bones[:]], [sq], ["bones", ("sqb", k2 % 2)])
                rs = rstd_from_ps(b2, TT, 1.0 / 64.0, k2 % 2)
                gsc = der[:, 3:4] if idx < 3 else spc(l, "k_gain")
                P.dve(_I("scalar_tensor_tensor", out=big[:, idx, ts], in0=qf, scalar=gsc, in1=rs, op0=ALU.mult, op1=ALU.mult),
                      r=[SK(k2 % 4), ("rsb", k2 % 2), "der", "spt"], w=[("big", idx, tt)])

            for pi in range(5):
                wp, wk = w_next("qkv%d" % pi)
                ncb = 2 if pi < 4 else 1
                for cb in range(ncb):
                    idx = pi * 2 + cb
                    for tt in range(NTT):
                        ts = slice(tt * TT, (tt + 1) * TT)
                        b = bank()
                        mm_group(b, [wp[:, kc, cb * 128:(cb + 1) * 128] for kc in range(8)], [hn[:, kc, ts] for kc in range(8)],
                                 [wk] + hn_keys(tt))
                        while pend:
                            tail(*pend.pop(0))
                        if idx >= 6:
                            P.act(_I("activation", out=big[:, idx, ts], in_=psb[b][:, :], func=AF.Identity),
                                  r=[PK(b)], w=[("big", idx, tt)])
                            continue
                        k2 = st.setdefault("qk", 0); st["qk"] = k2 + 1
                        P.act(_I("activation", out=sqb[:, k2 % 2, :], in_=psb[b][:, :], func=AF.Square), r=[PK(b)], w=[("sqb", k2 % 2)])
                        P.act(_I("activation", out=S(k2 % 4), in_=psb[b][:, :], func=AF.Identity), r=[PK(b)], w=[SK(k2 % 4)])
                        pend.append((idx, tt, k2))
            while pend:
                tail(*pend.pop(0))

        def stage_attn(l):
            acc = scr[:, 0:8, :].rearrange("p (a b) c -> p a (b c)", a=2)
            ACCK = [SK(i) for i in range(8)]
            DEPTH_A = 2
            PTS = (0, 1, 5)
            SUBK = ([("smb", a, hh) for a in PTS for hh in range(2)] + [(SK(a), hh) for a in range(9, 12) for hh in range(2)]
                    + [("smbv", q) for q in range(4)])
            WHOLEK = [("smb", a) for a in (0, 1, 2, 3, 5)] + [SK(9), SK(10), SK(11)]
            P.dve(_I("memset", dummy[:, 0:1], 0.0), w=WHOLEK + SUBK + ["dummy"])

            def vtile(q):
                return smallb[:, 2 + q // 2, (q % 2) * 256:(q % 2) * 256 + 256]
            for q in range(4):
                P.dve(_I("memset", vtile(q)[:, 64:128], 1.0), w=[("smbv", q)])
                P.dve(_I("memset", vtile(q)[:, 192:256], 1.0), w=[("smbv", q)])
            for c in range(3):
                qk_ = [("big", c, tt) for tt in range(NTT)]
                kk_ = [("big", 3 + c, tt) for tt in range(NTT)]
                vk_ = [("big", 6 + c, tt) for tt in range(NTT)]
                its = []
                for p, (w, d) in enumerate(PATTERNS):
                    nb = (SEQ // d) // 128
                    for r in range(d):
                        for bq in range(nb):
                            its.append((p, d, r, bq))

                def stage_a(n):
                    p, d, r, bq = its[n]
                    s0 = bq * 128 * d + r
                    tok = slice(s0, s0 + 127 * d + 1, d)
                    q4 = n % 4
                    vt = vtile(q4)
                    bvb = 4 + n % 2
                    pv_ = psb[bvb][:, 0:128]
                    P.pe(_I("matmul", pv_, big[:, 6 + c, tok], ident[:], start=True, stop=True), r=vk_ + ["ident"], w=[PK(bvb)])
                    P.act(_I("activation", out=vt.rearrange("p (a b) -> p a b", a=2)[:, :, 0:64],
                             in_=pv_.rearrange("p (a b) -> p a b", a=2), func=AF.Identity), r=[PK(bvb)], w=[("smbv", q4)])
                    W_ = 256 if bq > 0 else 128
                    pslot = 9 + (n % 3)
                    pe_ = S(pslot).bitcast(BF16)[:, 0:512]
                    ptslot = PTS[n % 3]
                    pT = smallb[:, ptslot, :]
                    for hh in range(2):
                        rows = slice(hh * 64, (hh + 1) * 64)
                        sb_ = 2 * (n % 2) + hh
                        sc = psb[sb_][:, 0:256]
                        P.pe(_I("matmul", sc[:, 0:128], big[rows, 3 + c, tok], big[rows, c, tok], start=True, stop=True),
                             r=qk_ + kk_, w=[PK(sb_)])
                        if bq > 0:
                            tokp = slice(s0 - 128 * d, s0 - d + 1, d)
                            P.pe(_I("matmul", sc[:, 128:256], big[rows, 3 + c, tokp], big[rows, c, tok], start=True, stop=True),
                                 r=qk_ + kk_, w=[PK(sb_)])
                        P.act(_I("activation", out=pe_[:, hh * 256:hh * 256 + W_], in_=sc[:, 0:W_], func=AF.Exp),
                              r=[PK(sb_)], w=[(SK(pslot), hh)])
                    P.dve(_I("tensor_tensor", out=pT.rearrange("p (a b) -> p a b", a=2)[:, :, 0:W_],
                             in0=pe_.rearrange("p (a b) -> p a b", a=2)[:, :, 0:W_],
                             in1=Eb[:, p * 6 + 2 * c:p * 6 + 2 * c + 2, 0:W_], op=ALU.mult),
                          r=[(SK(pslot), 0), (SK(pslot), 1), "Eb"], w=[("smb", ptslot, 0), ("smb", ptslot, 1)])

                def stage_b(n):
                    p, d, r, bq = its[n]
                    s0 = bq * 128 * d + r
                    tok = slice(s0, s0 + 127 * d + 1, d)
                    nblk = 2 if bq > 0 else 1
                    q4 = n % 4
                    pq4 = (n - 1) % 4
                    ptslot = PTS[n % 3]
                    pT = smallb[:, ptslot, :]
                    bob = 6 + n % 2
                    for hh in range(2):
                        o_ = psb[bob][:, hh * 128:(hh + 1) * 128]
                        vl = [vtile(q4)[:, hh * 128:(hh + 1) * 128]]
                        rk = [("smbv", q4), ("smb", ptslot, hh)]
                        if bq > 0:
                            vl.append(vtile(pq4)[:, hh * 128:(hh + 1) * 128])
                            rk.append(("smbv", pq4))
                        for kb in range(nblk):
                            P.pe(_I("matmul", o_, vl[kb], pT[:, hh * 256 + kb * 128: hh * 256 + (kb + 1) * 128],
                                    start=(kb == 0), stop=(kb == nblk - 1)), r=rk, w=[PK(bob)])
                    ov = psb[bob][:, 0:256].rearrange("p (a b) -> p a b", a=2)
                    av = acc[:, :, tok]
                    if p == 0:
                        P.act(_I("activation", out=av, in_=ov, func=AF.Identity), r=[PK(bob)], w=ACCK)
                    else:
                        P.dve(_I("tensor_tensor", out=av, in0=ov, in1=av, op=ALU.add), r=[PK(bob)] + ACCK, w=ACCK)

                N_ = len(its)
                for n in range(min(DEPTH_A, N_)):
                    stage_a(n)
                for n in range(N_):
                    if n + DEPTH_A < N_:
                        stage_a(n + DEPTH_A)
                    stage_b(n)
                tmp = S(8)
                for tt in range(NTT):
                    ts = slice(tt * TT, (tt + 1) * TT)
                    P.dve(_I("reciprocal", out=tmp[0:64, :], in_=acc[64:128, 0, ts]), r=ACCK, w=[SK(8)])
                    P.dve(_I("tensor_tensor", out=big[0:64, c, ts], in0=acc[0:64, 0, ts], in1=tmp[0:64, :], op=ALU.mult),
                          r=ACCK + [SK(8)], w=[("big", c, tt)])
                    P.dve(_I("reciprocal", out=tmp[0:64, :], in_=acc[64:128, 1, ts]), r=ACCK + [SK(8)], w=[SK(8)])
                    P.dve(_I("tensor_tensor", out=big[64:128, c, ts], in0=acc[0:64, 1, ts], in1=tmp[0:64, :], op=ALU.mult),
                          r=ACCK + [SK(8)], w=[("big", c, tt)])
            P.dve(_I("memset", dummy[:, 1:2], 0.0), r=SUBK, w=WHOLEK + ["dummy"])

        def stage_mem(l, s):
            memf = scr[:, 0:4, :].rearrange("p a b -> p (a b)").rearrange("p (c m) -> p c m", c=8)
            MK = [SK(i) for i in range(4)]
            MNK = [("smb", i) for i in range(4)]; KNK = [SK(4), SK(5)]; VMK = [SK(6), SK(7)]
            P.dma(_I("dma_start", out=memf, in_=memT[s].rearrange("(c p) m -> p c m", p=128)), "memf", w=MK)
            b = bank()
            for c in range(8):
                sq = sqb[:, c % 2, 0:MEM]
                P.act(_I("activation", out=sq, in_=memf[:, c, :], func=AF.Square), r=MK, w=[("sqb", c % 2)])
                P.pe(_I("matmul", psb[b][:, 0:MEM], ones[:], sq, start=(c == 0), stop=(c == 7)),
                     r=[("sqb", c % 2), "ones"], w=[PK(b)])
            rs = rstd_from_ps(b, MEM, 1.0 / D, 0)
            for c in range(8):
                P.dve(_I("scalar_tensor_tensor", out=memn[:, c, :], in0=memf[:, c, :], scalar=spc(l, "g_memkv", c),
                                                            in1=rs, op0=ALU.mult, op1=ALU.mult),
                      r=MK + [("rsb", 0), "spt"], w=MNK)
            for i in range(4):
                wp, wk = w_next("mk%d" % i)
                bj = []
                for j in range(2):
                    b = bank(); bj.append(b)
                    mm_group(b, [wp[:, kc, j * 128:(j + 1) * 128] for kc in range(8)], [memn[:, kc, :] for kc in range(8)],
                             [wk] + MNK, N=MEM)
                b2 = bank()
                for j in range(2):
                    sq = sqb[:, j, 0:MEM]
                    P.act(_I("activation", out=sq, in_=psb[bj[j]][:, 0:MEM], func=AF.Square),
                          r=[PK(bj[j])], w=[("sqb", j)])
                    P.pe(_I("matmul", psb[b2][:, 0:MEM], ones[:], sq, start=(j == 0), stop=(j == 1)),
                         r=[("sqb", j), "ones"], w=[PK(b2)])
                rs = rstd_from_ps(b2, MEM, 1.0 / 256.0, 1)
                for j in range(2):
                    P.dve(_I("scalar_tensor_tensor",
                        out=kn[:, 2 * i + j, :], in0=psb[bj[j]][:, 0:MEM], scalar=spc(l, "mkg", j), in1=rs, op0=ALU.mult, op1=ALU.mult),
                        r=[PK(bj[j]), ("rsb", 1), "spt"], w=KNK)
            for i in range(4):
                wp, wk = w_next("mv%d" % i)
                for mb in range(2):
                    b = bank()
                    mm_group(b, [memn[:, kc, mb * 128:(mb + 1) * 128] for kc in range(8)], [wp[:, kc, :] for kc in range(8)],
                             [wk] + MNK, N=256)
                    P.act(_I("activation", out=vmem[:, mb, i * 256:(i + 1) * 256], in_=psb[b][:, 0:256],
                                                                  func=AF.Identity), r=[PK(b)], w=VMK)
            sqbufs = [(sqb[:, 0, :], ("sqb", 0)), (sqb[:, 1, :], ("sqb", 1)), (smallb[:, 4, :], ("smb", 4)), (smallb[:, 5, :], ("smb", 5))]
            pendq = []

            def qtail(i, tt, bj, n):
                ts = slice(tt * TT, (tt + 1) * TT)
                b2 = bank()
                for j in range(2):
                    sq, sqk = sqbufs[(n % 2) * 2 + j]
                    P.pe(_I("matmul", psb[b2][:, :], ones[:], sq, start=(j == 0), stop=(j == 1)), r=[sqk, "ones"], w=[PK(b2)])
                slot = n % 2
                rs = rstd_from_ps(b2, TT, 1.0 / 256.0, slot)
                for j in range(2):
                    P.dve(_I("scalar_tensor_tensor", out=big[:, 2 * i + j, ts], in0=psb[bj[j]][:, :], scalar=der[:, 4 + j:5 + j],
                             in1=rs, op0=ALU.mult, op1=ALU.mult), r=[PK(bj[j]), ("rsb", slot), "der"], w=[("big", 2 * i + j, tt)])

            nq = 0
            for i in range(4):
                wp, wk = w_next("mq%d" % i)
                for tt in range(NTT):
                    ts = slice(tt * TT, (tt + 1) * TT)
                    bj = []
                    for j in range(2):
                        b = bank(); bj.append(b)
                        mm_group(b, [wp[:, kc, j * 128:(j + 1) * 128] for kc in range(8)], [hn[:, kc, ts] for kc in range(8)],
                                 [wk] + hn_keys(tt))
                    while pendq:
                        qtail(*pendq.pop(0))
                    for j in range(2):
                        sq, sqk = sqbufs[(nq % 2) * 2 + j]
                        P.act(_I("activation", out=sq, in_=psb[bj[j]][:, :], func=AF.Square), r=[PK(bj[j])], w=[sqk])
                    pendq.append((i, tt, bj, nq))
                    nq += 1
            while pendq:
                qtail(*pendq.pop(0))
            for tt in range(NTT):
                ts = slice(tt * TT, (tt + 1) * TT)
                for i in range(4):
                    pts = []
                    for mb in range(2):
                        b = bank()
                        mm_group(b, [kn[:, 2 * i + j, mb * 128:(mb + 1) * 128] for j in range(2)],
                                 [big[:, 2 * i + j, ts] for j in range(2)], KNK + [("big", 2 * i + j, tt) for j in range(2)])
                        pT = smallb[:, mb + 2 * (i % 2), :]
                        pk = ("smb", mb + 2 * (i % 2))
                        P.act(_I("activation", out=pT, in_=psb[b][:, :], func=AF.Exp), r=[PK(b)], w=[pk])
                        pts.append((pT, pk))
                    bd_ = bank()
                    mm_group(bd_, [ones[:], ones[:]], [pts[0][0], pts[1][0]], ["ones", pts[0][1], pts[1][1]])
                    rd = S(8 + (i % 2))
                    P.act(_I("activation", out=rd, in_=psb[bd_][:, :], func=AF.Identity), r=[PK(bd_)],
                          w=[SK(8 + (i % 2))])
                    P.dve(_I("reciprocal", out=rd, in_=rd), r=[SK(8 + (i % 2))], w=[SK(8 + (i % 2))])
                    for j in range(2):
                        bn = bank()
                        mm_group(bn, [vmem[:, mb, i * 256 + j * 128: i * 256 + (j + 1) * 128] for mb in range(2)],
                                 [pts[0][0], pts[1][0]], VMK + [pts[0][1], pts[1][1]])
                        P.dve(_I("tensor_tensor", out=hn[:, 2 * i + j, ts], in0=psb[bn][:, :],
                                                                                      in1=rd, op=ALU.mult),
                              r=[PK(bn), SK(8 + (i % 2))], w=[("hn", 2 * i + j, tt)])
            for i in range(4):
                wp, wk = w_next("mo%d" % i)
                for tt in range(NTT):
                    ts = slice(tt * TT, (tt + 1) * TT)
                    for m in range(2):
                        b = bank()
                        mm_group(b, [wp[:, kc, m * 128:(m + 1) * 128] for kc in range(8)], [hn[:, kc, ts] for kc in range(8)],
                                 [wk] + hn_keys(tt))
                        resid_add(b, 2 * i + m, tt)

        def stage_ffn(l):
            for (a, bnd) in FFN_PARTS:
                for j in range(a, bnd):
                    jj = j - a
                    wp, wk = w_next("up%d" % j)
                    gbuf = scr[:, 0:2, :].rearrange("p a b -> p (a b)")
                    P.dve(_I("memset", gbuf[:, 0:2], 0.0), w=[SK(0)])
                    for tt in range(NTT):
                        ts = slice(tt * TT, (tt + 1) * TT)
                        bg = bank(); bu = bank()
                        mm_group(bg, [wp[:, kc, 0:128] for kc in range(8)], [hn[:, kc, ts] for kc in range(8)], [wk] + hn_keys(tt))
                        mm_group(bu, [wp[:, kc, 128:256] for kc in range(8)], [hn[:, kc, ts] for kc in range(8)], [wk] + hn_keys(tt))
                        P.act(_I("activation", out=gbuf[:, 2:2 + TT], in_=psb[bg][:, :], func=AF.Identity),
                              r=[PK(bg)], w=[SK(0), SK(1)])
                        n_it = st.setdefault("fi", 0); st["fi"] = n_it + 1
                        tslot = 2 + (n_it % 2)
                        t1 = S(tslot)
                        P.dve(_I("tensor_scalar", out=t1, in0=gbuf[:, 2:2 + TT], scalar1=spc(l, "fcw", 0 * 22 + j),
                                                               scalar2=spc(l, "fcb", j), op0=ALU.mult, op1=ALU.add),
                              r=[SK(0), SK(1), "spt"], w=[SK(tslot)])
                        for k in range(1, 3):
                            P.dve(_I("scalar_tensor_tensor", out=t1, in0=gbuf[:, 2 - k:2 - k + TT],
                                                                               scalar=spc(l, "fcw", k * 22 + j), in1=t1,
                                                                               op0=ALU.mult, op1=ALU.add),
                                  r=[SK(0), SK(1), SK(tslot), "spt"], w=[SK(tslot)])
                        P.dve(_I("tensor_copy", out=gbuf[:, 0:2], in_=gbuf[:, TT:TT + 2]), r=[SK(0), SK(1)], w=[SK(0)])
                        P.act(_I("activation", out=t1, in_=t1, func=AF.Gelu_apprx_tanh), r=[SK(tslot)], w=[SK(tslot)])
                        P.dve(_I("tensor_tensor", out=big[:, jj, ts], in0=psb[bu][:, :], in1=t1,
                                                                                    op=ALU.mult),
                              r=[PK(bu), SK(tslot)], w=[("big", jj, tt)])
                K = bnd - a
                for i in range(4):
                    wp, wk = w_next("dn%d_%d" % (a, i))
                    for tt in range(NTT):
                        ts = slice(tt * TT, (tt + 1) * TT)
                        for m in range(2):
                            b = bank()
                            mm_group(b, [wp[:, k, m * 128:(m + 1) * 128] for k in range(K)], [big[:, k, ts] for k in range(K)],
                                     [wk] + [("big", k, tt) for k in range(K)])
                            resid_add(b, 2 * i + m, tt)

        w_issue_upto(LOOKAHEAD)
        stages = ["norm1", "lru", "pool", "woA", "qkv", "attn", "woB", "norm2", "mem", "norm3", "ffn"]
        n_stage = len(stages) if debug_stage is None else stages.index(debug_stage) + 1
        for s in range(nseq):
            for tt in range(NTT):
                P.dma(_I("dma_start", out=h[:, :, tt * TT:(tt + 1) * TT],
                         in_=xT[s, :, tt * TT:(tt + 1) * TT].rearrange("(c p) t -> p c t", p=128)), "hx%d" % tt,
                      w=[("h", c, tt) for c in range(8)])
            for l in range(nlayers):
                par = 0
                layer_setup(l, par)
                fns = [lambda: rmsnorm_h(l, "g_mix"), lambda: stage_lru(l, par), lambda: stage_pool(l, par),
                       lambda: stage_wout("woA", [0, 1, 2, 3, 4]), lambda: stage_qkv(l), lambda: stage_attn(l),
                       lambda: stage_wout("woB", [0, 1, 2]), lambda: rmsnorm_h(l, "g_mem"), lambda: stage_mem(l, s),
                       lambda: rmsnorm_h(l, "g_ffn"), lambda: stage_ffn(l)]
                for f in fns[:n_stage]:
                    f()
            for tt in range(NTT):
                P.dma(_I("dma_start", out=outT[s, :, tt * TT:(tt + 1) * TT].rearrange("(c p) t -> p c t", p=128),
                         in_=h[:, :, tt * TT:(tt + 1) * TT]), "ho%d" % tt,
                      r=[("h", c, tt) for c in range(8)], w=[("outT", tt)])
        if debug_stage is not None:
            dbg_big = nc.dram_tensor("dbg_big", [128, 9 * SEQ], F32, kind="ExternalOutput").ap()
            dbg_hn = nc.dram_tensor("dbg_hn", [128, 8 * SEQ], F32, kind="ExternalOutput").ap()
            P.dma(_I("dma_start", out=dbg_big, in_=big[:].rearrange("p a b -> p (a b)")), "dbg",
                  r=[("big", c, tt) for c in range(9) for tt in range(NTT)], w=["dbgk"], q="pool")
            P.dma(_I("dma_start", out=dbg_hn, in_=hn[:].rearrange("p a b -> p (a b)")), "dbg",
                  r=[("hn", c, tt) for c in range(8) for tt in range(NTT)], w=["dbgk"], q="pool")
        P.op("sp", lambda e: None, reads=[("outT", tt) for tt in range(NTT)] + ["dbgk"])
        P.emit()
    return nc


_CACHE = {}


def host_prepare(inp, nlayers=DEPTH):
    pieces = [weight_pieces_layer(inp, l) for l in range(DEPTH)]
    meta = [(nm, k, n) for nm, k, n, _ in pieces[0]]
    wst = np.concatenate([a for l in range(DEPTH) for _, _, _, a in pieces[l]], axis=1)
    wst = np.ascontiguousarray(wst, dtype=np.float32)
    E, cst = const_tables()
    shared = {"wst": wst, "spd": small_params(inp), "bdd": blockdiag_params(inp), "Ed": E, "cstd": cst}
    return meta, shared


def kernel(**inputs):
    inp = {k: np.asarray(v) for k, v in inputs.items()}
    x = inp["x"]; mem = inp["mem"]
    B = x.shape[0]
    nseq = B // NCORE
    meta, shared = host_prepare(inp)
    key = (nseq, DEPTH)
    nc = build_program(nseq=nseq, nlayers=DEPTH, piece_meta=meta)
    xT = np.ascontiguousarray(x.transpose(0, 2, 1))
    memT = np.ascontiguousarray(mem.transpose(0, 2, 1))
    in_maps = []
    for c in range(NCORE):
        m = dict(shared)
        m["xT"] = xT[c * nseq:(c + 1) * nseq]
        m["memT"] = memT[c * nseq:(c + 1) * nseq]
        in_maps.append(m)
    res = run_bass_kernel_spmd(nc, in_maps, core_ids=list(range(NCORE)))
    outT = np.concatenate([r["outT"] for r in res.results], axis=0)
    return np.ascontiguousarray(outT.transpose(0, 2, 1)).astype(np.float32)
```

```python
import bisect
from contextlib import ExitStack

import numpy as np
import concourse.bass as bass
import concourse.mybir as mybir
from concourse.bass_utils import run_bass_kernel_spmd

F32 = mybir.dt.float32
BF16 = mybir.dt.bfloat16
AF = mybir.ActivationFunctionType
ALU = mybir.AluOpType

D = 1024
SEQ = 2048
DEPTH = 4
NCORE = 8
MEM = 256
DFF = 2816
INW = 2176
EPS = 1e-6
TT = 512
NTT = SEQ // TT
PATTERNS = ((128, 1), (512, 4), (2048, 16))
NSLOT = 5
SLOT_E = 2048
LOOKAHEAD = 3
FFN_PARTS = ((0, 8), (8, 15), (15, 22))
SEM_LIMIT = 30000


def _I(m, *a, **k):
    return (m, a, k)


class _Op:
    __slots__ = ("eng", "fn", "deps", "sig", "semi", "val", "dma", "dsem", "dval", "idx")


class Prog:
    ENGS = ("pe", "act", "dve", "pool", "sp")

    def __init__(self, nc, es):
        self.nc = nc
        self.es = es
        self.ops = []
        self.lastw = {}
        self.readers = {}
        self.dma_sems = {}
        self.n_sems = 0

    def _new_sem(self, name):
        self.n_sems += 1
        return self.es.enter_context(self.nc.semaphore(name))

    def op(self, eng, fn, reads=(), writes=(), dma=None):
        o = _Op()
        o.eng = eng; o.fn = fn; o.sig = False; o.dma = dma; o.idx = len(self.ops)
        deps = set()
        lastw = self.lastw; readers = self.readers
        for k in reads:
            w = lastw.get(k)
            if w is not None:
                deps.add(w)
        for k in writes:
            w = lastw.get(k)
            if w is not None:
                deps.add(w)
            rl = readers.get(k)
            if rl:
                deps.update(rl)
        if eng == "pe" and dma is None:
            deps = {d for d in deps if not (d.eng == "pe" and d.dma is None)}
        o.deps = deps
        for d in deps:
            d.sig = True
        for k in writes:
            lastw[k] = o
            readers[k] = []
        for k in reads:
            rl = readers.get(k)
            if rl is None:
                readers[k] = [o]
            else:
                rl.append(o)
        self.ops.append(o)
        return o

    def pe(self, fn, r=(), w=()): return self.op("pe", fn, r, w)
    def act(self, fn, r=(), w=()): return self.op("act", fn, r, w)
    def dve(self, fn, r=(), w=()): return self.op("dve", fn, r, w)
    def pool(self, fn, r=(), w=()): return self.op("pool", fn, r, w)
    def dma(self, fn, sem, r=(), w=(), q="sp"): return self.op(q, fn, r, w, dma=sem)

    def emit(self):
        nc = self.nc
        cur = {}
        for o in self.ops:
            if o.dma is not None:
                ent = self.dma_sems.get(o.dma)
                if ent is None:
                    ent = [self._new_sem("d_" + o.dma), 0]
                    self.dma_sems[o.dma] = ent
                ent[1] += 16
                o.dsem = ent[0]; o.dval = ent[1]
                continue
            if not o.sig:
                continue
            ent = cur.get(o.eng)
            if ent is None or ent[1] >= SEM_LIMIT:
                ent = [self._new_sem("e_%s_%d" % (o.eng, self.n_sems)), 0]
                cur[o.eng] = ent
            ent[1] += 1
            o.semi = ent[0]; o.val = ent[1]
        dma_hist = {}
        for o in self.ops:
            if o.dma is not None:
                dma_hist.setdefault(o.dma, []).append((o.idx, o.dval))
        dma_idx = {k: [a for a, _ in v] for k, v in dma_hist.items()}
        per_eng = {e: [] for e in self.ENGS}
        for o in self.ops:
            per_eng[o.eng].append(o)

        def run(engname, e):
            waited = {}
            for o in per_eng[engname]:
                need = {}
                for d in o.deps:
                    if d.dma is not None:
                        j = bisect.bisect_left(dma_idx[d.dma], o.idx) - 1
                        sem, val = d.dsem, dma_hist[d.dma][j][1]
                    else:
                        sem, val = d.semi, d.val
                    key = id(sem)
                    if key not in need or need[key][1] < val:
                        need[key] = (sem, val)
                for key, (sem, val) in need.items():
                    if waited.get(key, 0) < val:
                        e.wait_ge(sem, val)
                        waited[key] = val
                f = o.fn
                ins = getattr(e, f[0])(*f[1], **f[2]) if isinstance(f, tuple) else f(e)
                if ins is None:
                    continue
                if o.dma is not None:
                    ins.then_inc(o.dsem, 16)
                elif o.sig:
                    ins.then_inc(o.semi, 1)

        with nc.Block() as block:
            @block.tensor
            def _(e): run("pe", e)

            @block.scalar
            def _(e): run("act", e)

            @block.vector
            def _(e): run("dve", e)

            @block.gpsimd
            def _(e): run("pool", e)

            @block.sync
            def _(e): run("sp", e)


def _piece(W, rows, cols):
    blk = np.stack([np.concatenate([W[r:r + 128, c:c + 128] for c in cols], axis=1) for r in rows], axis=1)
    return blk.reshape(128, -1)


def weight_pieces_layer(inp, l):
    w_in = inp["w_in"][l]; w_out = inp["w_out"][l]; w_q = inp["w_q_mem"][l]; w_kv = inp["w_kv_mem"][l]
    w_o = inp["w_o_mem"][l]; w_up = inp["w_up"][l]; w_down = inp["w_down"][l]
    k8 = [i * 128 for i in range(8)]
    out = []
    for j in range(3):
        out.append(("lru%d" % j, 8, 256, _piece(w_in, k8, [1408 + j * 128, 1792 + j * 128])))
    out.append(("pool", 8, 256, _piece(w_in, k8, [0, 128])))
    rowsA = [640, 768, 896, 0, 128]
    for i in range(4):
        out.append(("woA%d" % i, 5, 256, _piece(w_out, rowsA, [i * 256, i * 256 + 128])))
    qkv_cols = [256 + i * 128 for i in range(9)]
    for i in range(0, 9, 2):
        cs = qkv_cols[i:i + 2]
        out.append(("qkv%d" % (i // 2), 8, 128 * len(cs), _piece(w_in, k8, cs)))
    rowsB = [256, 384, 512]
    for i in range(4):
        out.append(("woB%d" % i, 3, 256, _piece(w_out, rowsB, [i * 256, i * 256 + 128])))
    for i in range(4):
        out.append(("mk%d" % i, 8, 256, _piece(w_kv, k8, [i * 256, i * 256 + 128])))
    for i in range(4):
        out.append(("mv%d" % i, 8, 256, _piece(w_kv, k8, [1024 + i * 256, 1024 + i * 256 + 128])))
    for i in range(4):
        out.append(("mq%d" % i, 8, 256, _piece(w_q, k8, [i * 256, i * 256 + 128])))
    for i in range(4):
        out.append(("mo%d" % i, 8, 256, _piece(w_o, k8, [i * 256, i * 256 + 128])))
    for (a, b) in FFN_PARTS:
        for j in range(a, b):
            out.append(("up%d" % j, 8, 256, _piece(w_up, k8, [j * 128, DFF + j * 128])))
        rows = [j * 128 for j in range(a, b)]
        for i in range(4):
            out.append(("dn%d_%d" % (a, i), b - a, 256, _piece(w_down, rows, [i * 256, i * 256 + 128])))
    return out


class SP:
    names = [("g_mix", 8), ("g_mem", 8), ("g_memkv", 8), ("g_ffn", 8), ("pool_scale", 2), ("q_gain", 1), ("k_gain", 1),
             ("lcw", 12), ("lcb", 3), ("lba", 3), ("lbx", 3), ("llam", 3), ("mqg", 2), ("mkg", 2),
             ("fcw", 66), ("fcb", 22)]
    off = {}
    n = 0
    for _nm, _w in names:
        off[_nm] = n
        n += _w
    PER_LAYER = n


def fm(v, nchunk):
    return np.ascontiguousarray(v.reshape(nchunk, 128).T)


def small_params(inp):
    t = np.zeros((128, DEPTH, SP.PER_LAYER), np.float32)
    for l in range(DEPTH):
        def put(name, arr):
            o = SP.off[name]
            t[:, l, o:o + arr.shape[1]] = arr
        put("g_mix", fm(inp["norm_mix"][l], 8)); put("g_mem", fm(inp["norm_mem"][l], 8))
        put("g_memkv", fm(inp["norm_memkv"][l], 8)); put("g_ffn", fm(inp["norm_ffn"][l], 8))
        put("pool_scale", fm(inp["pool_scale"][l], 2))
        put("q_gain", np.tile(inp["q_gain"][l], 2)[:, None]); put("k_gain", np.tile(inp["k_gain"][l], 2)[:, None])
        put("lcw", np.concatenate([fm(inp["lru_conv_w"][l, k], 3) for k in range(4)], axis=1))
        put("lcb", fm(inp["lru_conv_b"][l], 3)); put("lba", fm(inp["lru_ba"][l], 3)); put("lbx", fm(inp["lru_bx"][l], 3))
        put("llam", fm(inp["lru_lambda"][l], 3))
        put("mqg", fm(inp["mq_gain"][l], 2)); put("mkg", fm(inp["mk_gain"][l], 2))
        put("fcw", np.concatenate([fm(inp["ffn_conv_w"][l, k], 22) for k in range(3)], axis=1))
        put("fcb", fm(inp["ffn_conv_b"][l], 22))
    return t.reshape(128, DEPTH * SP.PER_LAYER)


def blockdiag_params(inp):
    t = np.zeros((128, DEPTH, 8, 128), np.float32)
    for l in range(DEPTH):
        for c in range(2):
            for hh in range(2):
                t[hh * 64:(hh + 1) * 64, l, c, hh * 64:(hh + 1) * 64] = inp["pool_w"][l, 2 * c + hh]
        for c in range(3):
            for hh in range(2):
                t[hh * 64:(hh + 1) * 64, l, 2 + c, hh * 64:(hh + 1) * 64] = inp["lru_wa"][l, 2 * c + hh]
                t[hh * 64:(hh + 1) * 64, l, 5 + c, hh * 64:(hh + 1) * 64] = inp["lru_wx"][l, 2 * c + hh]
    return t.reshape(128, DEPTH * 8 * 128)


def const_tables():
    slopes = np.array([2.0 ** (-8.0 * (h + 1) / 6) for h in range(6)], np.float64)
    j = np.arange(128)[:, None].astype(np.float64); i = np.arange(128)[None, :].astype(np.float64)
    E = np.zeros((128, 18, 2, 128), np.float64)
    for p, (w, d) in enumerate(PATTERNS):
        for h in range(6):
            S = slopes[h] * d
            dist_c = i - j
            E[:, p * 6 + h, 0, :] = np.where(dist_c >= 0, np.exp(-S * np.maximum(dist_c, 0)), 0.0)
            dist_p = 128 + i - j
            E[:, p * 6 + h, 1, :] = np.where(dist_p <= 128, np.exp(-S * dist_p), 0.0)
    invc = np.zeros((128, 2, 16), np.float64); invw = np.zeros((128, 2), np.float64)
    wins = ((2, 4), (8, 16))
    t = np.arange(16)
    for c in range(2):
        for hh in range(2):
            w = wins[c][hh]
            invc[hh * 64:(hh + 1) * 64, c, :] = 1.0 / np.minimum(t + 1, w)
            invw[hh * 64:(hh + 1) * 64, c] = 1.0 / w
    cst = np.concatenate([invc.reshape(128, 32), invw], axis=1).astype(np.float32)
    return E.reshape(128, 18 * 256).astype(np.float32), cst


def build_program(nseq=4, nlayers=DEPTH, piece_meta=None, debug_stage=None):
    nc = bass.Bass("TRN2", target_bir_lowering=False)
    per_layer_elems = sum(k * n for _, k, n in piece_meta)
    xT = nc.dram_tensor("xT", [nseq, D, SEQ], F32, kind="ExternalInput").ap()
    memT = nc.dram_tensor("memT", [nseq, D, MEM], F32, kind="ExternalInput").ap()
    wst = nc.dram_tensor("wst", [128, nlayers * per_layer_elems], F32, kind="ExternalInput").ap()
    spd = nc.dram_tensor("spd", [128, DEPTH * SP.PER_LAYER], F32, kind="ExternalInput").ap()
    bdd = nc.dram_tensor("bdd", [128, DEPTH * 8 * 128], F32, kind="ExternalInput").ap()
    Ed = nc.dram_tensor("Ed", [128, 18 * 256], F32, kind="ExternalInput").ap()
    cstd = nc.dram_tensor("cstd", [128, 34], F32, kind="ExternalInput").ap()
    outT = nc.dram_tensor("outT", [nseq, D, SEQ], F32, kind="ExternalOutput").ap()

    es = ExitStack()
    with es:
        P = Prog(nc, es)
        sb = lambda name, shape, dt: es.enter_context(nc.sbuf_tensor(name, shape, dt))
        h = sb("h", [128, 8, SEQ], F32)
        hn = sb("hn", [128, 8, SEQ], BF16)
        big = sb("big", [128, 9, SEQ], BF16)
        wsl = sb("wsl", [128, NSLOT, SLOT_E], BF16)
        scr = sb("scr", [128, 12, TT], F32)
        Eb = sb("Eb", [128, 18, 256], BF16)
        spt = sb("spt", [128, DEPTH, SP.PER_LAYER], F32)
        bdt = sb("bdt", [128, 1, 8, 128], BF16)
        cst = sb("cst", [128, 34], F32)
        dummy = sb("dummyt", [128, 2], F32)
        der = sb("der", [128, 8], F32)
        ones = sb("ones", [128, 128], BF16)
        bones = sb("bones", [128, 128], BF16)
        ident = sb("ident", [128, 128], BF16)
        sqb = sb("sqb", [128, 2, TT], BF16)
        rsb = sb("rsb", [128, 2, TT], F32)
        smallb = sb("smallb", [128, 6, TT], BF16)
        idf = scr[:, 11, 0:128]
        memn = smallb[:, 0:4, :].rearrange("p a b -> p (a b)").rearrange("p (c m) -> p c m", c=8)
        kn = scr[:, 4:6, :].rearrange("p a b -> p (a b)").bitcast(BF16).rearrange("p (c m) -> p c m", c=8)
        vmem = scr[:, 6:8, :].rearrange("p a b -> p (a b)").bitcast(BF16).rearrange("p (c m) -> p c m", c=2)
        psb = [es.enter_context(nc.psum_tensor("ps%d" % i, [128, TT], F32)) for i in range(8)]

        st = {"bank": 0}

        def bank():
            b = st["bank"]
            st["bank"] = (b + 1) % 8
            return b

        def PK(b): return ("ps", b)

        stream = []
        for s in range(nseq):
            for l in range(nlayers):
                off = l * per_layer_elems
                for i, (nm, kc, n) in enumerate(piece_meta):
                    stream.append((nm, off, kc, n))
                    off += kc * n
        wstate = {"issued": 0, "next": 0}

        def w_issue_upto(k):
            while wstate["issued"] < min(k, len(stream)):
                i = wstate["issued"]
                nm, off, kc, n = stream[i]
                slot = i % NSLOT
                src = wst[:, off:off + kc * n]
                dst = wsl[:, slot, 0:kc * n]
                P.dma(_I("dma_start", out=dst, in_=src), "w%d" % slot, w=[("w", slot)], q="pool")
                wstate["issued"] += 1

        def w_next(expect):
            i = wstate["next"]
            nm, off, kc, n = stream[i]
            assert nm == expect, (nm, expect)
            w_issue_upto(i + 1 + LOOKAHEAD)
            wstate["next"] += 1
            slot = i % NSLOT
            return wsl[:, slot, 0:kc * n].rearrange("p (k n) -> p k n", k=kc), ("w", slot)

        P.dma(_I("dma_start", out=spt[:].rearrange("p a b -> p (a b)"), in_=spd), "c0", w=["spt"])
        P.dma(_I("dma_start", out=cst[:], in_=cstd), "c1", w=["cst"])
        P.dma(_I("dma_start", out=Eb[:].rearrange("p a b -> p (a b)"), in_=Ed), "c2", w=["Eb"], q="pool")
        P.dve(_I("memset", ones[:], 1.0), w=["ones"])
        P.dve(_I("memset", bones[:], 0.0), w=["bones"])
        P.dve(_I("memset", bones[0:64, 0:64], 1.0), w=["bones"])
        P.dve(_I("memset", bones[64:128, 64:128], 1.0), w=["bones"])
        P.pool(_I("memset", idf[:], 1.0), w=["idf"])
        P.pool(_I("affine_select", out=idf[:], in_=idf[:], pattern=[[-1, 128]], compare_op=ALU.is_equal,
                                         fill=0.0, base=0, channel_multiplier=1), r=["idf"], w=["idf"])
        P.dve(_I("tensor_copy", out=ident[:], in_=idf[:]), r=["idf"], w=["ident"])

        def spc(l, name, j=0, w=1):
            o = SP.off[name] + j
            return spt[:, l, o:o + w]

        def mm_group(b, lhs, rhs, rkeys, N=TT, M=128, po=0, co=0, tp=None):
            ps = psb[b][po:po + M, co:co + N]
            n = len(lhs)
            for i in range(n):
                if tp is None:
                    P.pe(_I("matmul", ps, lhs[i], rhs[i], start=(i == 0), stop=(i == n - 1)), r=rkeys, w=[PK(b)])
                else:
                    P.pe(_I("matmul", ps, lhs[i], rhs[i], start=(i == 0), stop=(i == n - 1), tile_position=tp), r=rkeys, w=[PK(b)])

        def rstd_from_ps(b, N, inv_n, slot):
            rs = rsb[:, slot, 0:N]
            P.act(_I("activation", out=rs, in_=psb[b][:, 0:N], func=AF.Sqrt, bias=EPS, scale=inv_n),
                  r=[PK(b)], w=[("rsb", slot)])
            P.dve(_I("reciprocal", out=rs, in_=rs), r=[("rsb", slot)], w=[("rsb", slot)])
            return rs

        def rmsnorm_h(l, gname):
            for tt in range(NTT):
                ts = slice(tt * TT, (tt + 1) * TT)
                b = bank()
                for c in range(8):
                    sq = sqb[:, c % 2, :]
                    P.act(_I("activation", out=sq, in_=h[:, c, ts], func=AF.Square),
                          r=[("h", c, tt)], w=[("sqb", c % 2)])
                    P.pe(_I("matmul", psb[b][:, :], ones[:], sq, start=(c == 0), stop=(c == 7)),
                         r=[("sqb", c % 2), "ones"], w=[PK(b)])
                slot = tt % 2
                rs = rstd_from_ps(b, TT, 1.0 / D, slot)
                for c in range(8):
                    P.dve(_I("scalar_tensor_tensor", out=hn[:, c, ts], in0=h[:, c, ts], scalar=spc(l, gname, c),
                                                                      in1=rs, op0=ALU.mult, op1=ALU.mult),
                          r=[("h", c, tt), ("rsb", slot), "spt"], w=[("hn", c, tt)])

        def hn_keys(tt): return [("hn", c, tt) for c in range(8)]

        def resid_add(b, oc, tt):
            ts = slice(tt * TT, (tt + 1) * TT)
            P.dve(_I("tensor_tensor", out=h[:, oc, ts], in0=psb[b][:, :], in1=h[:, oc, ts], op=ALU.add),
                  r=[PK(b), ("h", oc, tt)], w=[("h", oc, tt)])

        def S(i): return scr[:, i, :]

        def SK(i): return ("scr", i)

        def layer_setup(l, par):
            src = bdd[:, l * 1024:(l + 1) * 1024]
            P.dma(_I("dma_start", out=bdt[:, par, :, :].rearrange("p a b -> p (a b)"), in_=src), "bd%d" % par,
                  w=[("bdt", par)], q="pool")
            P.act(_I("activation", out=der[:, 0:3], in_=spc(l, "llam", 0, 3), func=AF.Exp, scale=-1.0),
                  r=["spt"], w=["der"])
            P.act(_I("activation", out=der[:, 0:3], in_=der[:, 0:3], func=AF.Ln, bias=1.0), r=["der"], w=["der"])
            P.dve(_I("tensor_scalar", out=der[:, 0:3], in0=der[:, 0:3], scalar1=-8.0, scalar2=None, op0=ALU.mult),
                  r=["der"], w=["der"])
            P.dve(_I("tensor_scalar", out=der[:, 3:4], in0=spc(l, "q_gain"), scalar1=0.125, scalar2=None, op0=ALU.mult),
                  r=["spt", "der"], w=["der"])
            P.dve(_I("tensor_scalar", out=der[:, 4:6], in0=spc(l, "mqg", 0, 2), scalar1=1.0 / 16.0, scalar2=None,
                                            op0=ALU.mult), r=["spt", "der"], w=["der"])

        def stage_lru(l, par):
            xbuf = scr[:, 0:2, :].rearrange("p a b -> p (a b)")
            units = [(j, tt) for j in range(3) for tt in range(NTT)]
            wps = {}

            def a_unit(n):
                j, tt = units[n]
                if tt == 0:
                    wps[j] = w_next("lru%d" % j)
                    P.dve(_I("memset", xbuf[:, 0:3], 0.0), w=[SK(0)])
                wp, wk = wps[j]
                ts = slice(tt * TT, (tt + 1) * TT)
                gs = 2 if n % 2 == 0 else 10
                xs = 3 if n % 2 == 0 else 11
                bx = bank(); by = bank()
                mm_group(bx, [wp[:, kc, 0:128] for kc in range(8)], [hn[:, kc, ts] for kc in range(8)], [wk] + hn_keys(tt))
                mm_group(by, [wp[:, kc, 128:256] for kc in range(8)], [hn[:, kc, ts] for kc in range(8)], [wk] + hn_keys(tt))
                P.act(_I("activation", out=xbuf[:, 3:3 + TT], in_=psb[bx][:, :], func=AF.Identity), r=[PK(bx)], w=[SK(0), SK(1)])
                P.act(_I("activation", out=S(gs), in_=psb[by][:, :], func=AF.Gelu_apprx_tanh), r=[PK(by)], w=[SK(gs)])
                xc = S(xs)
                P.dve(_I("tensor_scalar", out=xc, in0=xbuf[:, 3:3 + TT], scalar1=spc(l, "lcw", 0 * 3 + j),
                         scalar2=spc(l, "lcb", j), op0=ALU.mult, op1=ALU.add), r=[SK(0), SK(1), "spt"], w=[SK(xs)])
                for k in range(1, 4):
                    P.dve(_I("scalar_tensor_tensor", out=xc, in0=xbuf[:, 3 - k:3 - k + TT], scalar=spc(l, "lcw", k * 3 + j),
                             in1=xc, op0=ALU.mult, op1=ALU.add), r=[SK(0), SK(1), SK(xs), "spt"], w=[SK(xs)])
                P.dve(_I("tensor_copy", out=xbuf[:, 0:3], in_=xbuf[:, TT:TT + 3]), r=[SK(0), SK(1)], w=[SK(0)])
                P.act(_I("activation", out=smallb[:, n % 2, :], in_=xc, func=AF.Identity), r=[SK(xs)], w=[("smb", n % 2)])

            def b_unit(n):
                j, tt = units[n]
                ts = slice(tt * TT, (tt + 1) * TT)
                gs = 2 if n % 2 == 0 else 10
                xs = 3 if n % 2 == 0 else 11
                xc = S(xs); gy = S(gs); xcb = smallb[:, n % 2, :]
                ba_ = bank(); bxg = bank()
                mm_group(ba_, [bdt[:, par, 2 + j, :]], [xcb], [("bdt", par), ("smb", n % 2)])
                mm_group(bxg, [bdt[:, par, 5 + j, :]], [xcb], [("bdt", par), ("smb", n % 2)])
                rg = S(4); ig = S(5); aa = S(6); t1 = S(7)
                P.act(_I("activation", out=rg, in_=psb[ba_][:, :], func=AF.Sigmoid, bias=spc(l, "lba", j)), r=[PK(ba_), "spt"], w=[SK(4)])
                P.act(_I("activation", out=ig, in_=psb[bxg][:, :], func=AF.Sigmoid, bias=spc(l, "lbx", j)), r=[PK(bxg), "spt"], w=[SK(5)])
                P.act(_I("activation", out=aa, in_=rg, func=AF.Exp, scale=der[:, j:j + 1]), r=[SK(4), "der"], w=[SK(6)])
                P.act(_I("activation", out=t1, in_=aa, func=AF.Square), r=[SK(6)], w=[SK(7)])
                P.act(_I("activation", out=t1, in_=t1, func=AF.Sqrt, bias=1.0, scale=-1.0), r=[SK(7)], w=[SK(7)])
                P.dve(_I("tensor_tensor", out=ig, in0=ig, in1=xc, op=ALU.mult), r=[SK(5), SK(xs)], w=[SK(5)])
                P.dve(_I("tensor_tensor", out=ig, in0=ig, in1=t1, op=ALU.mult), r=[SK(5), SK(7)], w=[SK(5)])
                hr = S(8 + (tt % 2))
                if tt == 0:
                    P.dve(_I("tensor_tensor_scan", out=hr, data0=aa, data1=ig, initial=0.0, op0=ALU.mult, op1=ALU.add),
                          r=[SK(6), SK(5)], w=[SK(8 + (tt % 2))])
                else:
                    prev = S(8 + ((tt - 1) % 2))
                    P.dve(_I("tensor_tensor_scan", out=hr, data0=aa, data1=ig, initial=prev[:, TT - 1:TT], op0=ALU.mult,
                             op1=ALU.add), r=[SK(6), SK(5), SK(8 + ((tt - 1) % 2))], w=[SK(8 + (tt % 2))])
                P.dve(_I("tensor_tensor", out=big[:, j, ts], in0=hr, in1=gy, op=ALU.mult),
                      r=[SK(8 + (tt % 2)), SK(gs)], w=[("big", j, tt)])

            N_ = len(units)
            a_unit(0)
            for n in range(N_):
                if n + 1 < N_:
                    a_unit(n + 1)
                b_unit(n)

        def stage_pool(l, par):
            wp, wk = w_next("pool")
            u = scr[:, 0:4, :].rearrange("p a b -> p (a b)")
            sA = scr[:, 4:8, :].rearrange("p a b -> p (a b)")
            sB = scr[:, 8:12, :].rearrange("p a b -> p (a b)")
            UK = [SK(i) for i in range(0, 4)]; AK = [SK(i) for i in range(4, 8)]; BK = [SK(i) for i in range(8, 12)]

            def shadd(dst, src, sh, lo, hi, dk, sk):
                P.dve(_I("tensor_copy", out=dst[lo:hi, 0:sh], in_=src[lo:hi, 0:sh]), r=sk, w=dk)
                P.dve(_I("tensor_tensor", out=dst[lo:hi, sh:SEQ], in0=src[lo:hi, sh:SEQ], in1=src[lo:hi, 0:SEQ - sh],
                                                op=ALU.add), r=sk, w=dk)

            for c in range(2):
                for tt in range(NTT):
                    ts = slice(tt * TT, (tt + 1) * TT)
                    b = bank()
                    mm_group(b, [wp[:, kc, c * 128:(c + 1) * 128] for kc in range(8)], [hn[:, kc, ts] for kc in range(8)],
                             [wk] + hn_keys(tt))
                    P.act(_I("activation", out=u[:, ts], in_=psb[b][:, :], func=AF.Identity),
                          r=[PK(b)], w=[SK(tt)])
                shadd(sA, u, 1, 0, 128, AK, UK)
                if c == 0:
                    shadd(sB, sA, 2, 64, 128, BK, AK)
                else:
                    shadd(sB, sA, 2, 0, 128, BK, AK)
                    shadd(sA, sB, 4, 0, 128, AK, BK)
                    shadd(sB, sA, 8, 64, 128, BK, AK)
                pl = big[:, 3 + c, :]
                plk = [("big", 3 + c, tt) for tt in range(NTT)]
                for (lo, hi, F, fk) in ((0, 64, sA, AK), (64, 128, sB, BK)):
                    P.dve(_I("scalar_tensor_tensor",
                        out=pl[lo:hi, 16:SEQ], in0=F[lo:hi, 16:SEQ], scalar=cst[lo:hi, 32 + c:33 + c], in1=u[lo:hi, 16:SEQ],
                        op0=ALU.mult, op1=ALU.subtract), r=fk + UK + ["cst"], w=plk)
                    P.dve(_I("tensor_tensor", out=F[lo:hi, 0:16], in0=F[lo:hi, 0:16],
                                                                        in1=cst[lo:hi, c * 16:(c + 1) * 16], op=ALU.mult),
                          r=fk + ["cst"], w=fk)
                    P.dve(_I("tensor_tensor", out=pl[lo:hi, 0:16], in0=F[lo:hi, 0:16],
                                                                        in1=u[lo:hi, 0:16], op=ALU.subtract),
                          r=fk + UK, w=plk)
                for tt in range(NTT):
                    ts = slice(tt * TT, (tt + 1) * TT)
                    b = bank()
                    mm_group(b, [bdt[:, par, c, :]], [big[:, 3 + c, ts]], [("bdt", par), ("big", 3 + c, tt)])
                    P.act(_I("activation", out=big[:, 3 + c, ts], in_=psb[b][:, :], func=AF.Identity,
                                                             scale=spc(l, "pool_scale", c)),
                          r=[PK(b), "spt"], w=[("big", 3 + c, tt)])

        def stage_wout(name, srcs):
            K = len(srcs)
            for i in range(4):
                wp, wk = w_next("%s%d" % (name, i))
                for tt in range(NTT):
                    ts = slice(tt * TT, (tt + 1) * TT)
                    for m in range(2):
                        b = bank()
                        mm_group(b, [wp[:, k, m * 128:(m + 1) * 128] for k in range(K)], [big[:, srcs[k], ts] for k in range(K)],
                                 [wk] + [("big", srcs[k], tt) for k in range(K)])
                        resid_add(b, 2 * i + m, tt)

        def stage_qkv(l):
            pend = []

            def tail(idx, tt, k2):
                ts = slice(tt * TT, (tt + 1) * TT)
                sq = sqb[:, k2 % 2, :]
                qf = S(k2 % 4)
                b2 = bank()
                mm_group(b2, [bones[:]], [sq], ["bones", ("sqb", k2 % 2)])
                rs = rstd_from_ps(b2, TT, 1.0 / 64.0, k2 % 2)
                gsc = der[:, 3:4] if idx < 3 else spc(l, "k_gain")
                P.dve(_I("scalar_tensor_tensor", out=big[:, idx, ts], in0=qf, scalar=gsc, in1=rs, op0=ALU.mult, op1=ALU.mult),
                      r=[SK(k2 % 4), ("rsb", k2 % 2), "der", "spt"], w=[("big", idx, tt)])

            for pi in range(5):
                wp, wk = w_next("qkv%d" % pi)
                ncb = 2 if pi < 4 else 1
                for cb in range(ncb):
                    idx = pi * 2 + cb
                    for tt in range(NTT):
                        ts = slice(tt * TT, (tt + 1) * TT)
                        b = bank()
                        mm_group(b, [wp[:, kc, cb * 128:(cb + 1) * 128] for kc in range(8)], [hn[:, kc, ts] for kc in range(8)],
                                 [wk] + hn_keys(tt))
                        while pend:
                            tail(*pend.pop(0))
                        if idx >= 6:
                            P.act(_I("activation", out=big[:, idx, ts], in_=psb[b][:, :], func=AF.Identity),
                                  r=[PK(b)], w=[("big", idx, tt)])
                            continue
                        k2 = st.setdefault("qk", 0); st["qk"] = k2 + 1
                        P.act(_I("activation", out=sqb[:, k2 % 2, :], in_=psb[b][:, :], func=AF.Square), r=[PK(b)], w=[("sqb", k2 % 2)])
                        P.act(_I("activation", out=S(k2 % 4), in_=psb[b][:, :], func=AF.Identity), r=[PK(b)], w=[SK(k2 % 4)])
                        pend.append((idx, tt, k2))
            while pend:
                tail(*pend.pop(0))

        def stage_attn(l):
            acc = scr[:, 0:8, :].rearrange("p (a b) c -> p a (b c)", a=2)
            ACCK = [SK(i) for i in range(8)]
            DEPTH_A = 2
            PTS = (0, 1, 5)
            SUBK = ([("smb", a, hh) for a in PTS for hh in range(2)] + [(SK(a), hh) for a in range(9, 12) for hh in range(2)]
                    + [("smbv", q) for q in range(4)])
            WHOLEK = [("smb", a) for a in (0, 1, 2, 3, 5)] + [SK(9), SK(10), SK(11)]
            P.dve(_I("memset", dummy[:, 0:1], 0.0), w=WHOLEK + SUBK + ["dummy"])

            def vtile(q):
                return smallb[:, 2 + q // 2, (q % 2) * 256:(q % 2) * 256 + 256]
            for q in range(4):
                P.dve(_I("memset", vtile(q)[:, 64:128], 1.0), w=[("smbv", q)])
                P.dve(_I("memset", vtile(q)[:, 192:256], 1.0), w=[("smbv", q)])
            for c in range(3):
                qk_ = [("big", c, tt) for tt in range(NTT)]
                kk_ = [("big", 3 + c, tt) for tt in range(NTT)]
                vk_ = [("big", 6 + c, tt) for tt in range(NTT)]
                its = []
                for p, (w, d) in enumerate(PATTERNS):
                    nb = (SEQ // d) // 128
                    for r in range(d):
                        for bq in range(nb):
                            its.append((p, d, r, bq))

                def stage_a(n):
                    p, d, r, bq = its[n]
                    s0 = bq * 128 * d + r
                    tok = slice(s0, s0 + 127 * d + 1, d)
                    q4 = n % 4
                    vt = vtile(q4)
                    bvb = 4 + n % 2
                    pv_ = psb[bvb][:, 0:128]
                    P.pe(_I("matmul", pv_, big[:, 6 + c, tok], ident[:], start=True, stop=True), r=vk_ + ["ident"], w=[PK(bvb)])
                    P.act(_I("activation", out=vt.rearrange("p (a b) -> p a b", a=2)[:, :, 0:64],
                             in_=pv_.rearrange("p (a b) -> p a b", a=2), func=AF.Identity), r=[PK(bvb)], w=[("smbv", q4)])
                    W_ = 256 if bq > 0 else 128
                    pslot = 9 + (n % 3)
                    pe_ = S(pslot).bitcast(BF16)[:, 0:512]
                    ptslot = PTS[n % 3]
                    pT = smallb[:, ptslot, :]
                    for hh in range(2):
                        rows = slice(hh * 64, (hh + 1) * 64)
                        sb_ = 2 * (n % 2) + hh
                        sc = psb[sb_][:, 0:256]
                        P.pe(_I("matmul", sc[:, 0:128], big[rows, 3 + c, tok], big[rows, c, tok], start=True, stop=True),
                             r=qk_ + kk_, w=[PK(sb_)])
                        if bq > 0:
                            tokp = slice(s0 - 128 * d, s0 - d + 1, d)
                            P.pe(_I("matmul", sc[:, 128:256], big[rows, 3 + c, tokp], big[rows, c, tok], start=True, stop=True),
                                 r=qk_ + kk_, w=[PK(sb_)])
                        P.act(_I("activation", out=pe_[:, hh * 256:hh * 256 + W_], in_=sc[:, 0:W_], func=AF.Exp),
                              r=[PK(sb_)], w=[(SK(pslot), hh)])
                    P.dve(_I("tensor_tensor", out=pT.rearrange("p (a b) -> p a b", a=2)[:, :, 0:W_],
                             in0=pe_.rearrange("p (a b) -> p a b", a=2)[:, :, 0:W_],
                             in1=Eb[:, p * 6 + 2 * c:p * 6 + 2 * c + 2, 0:W_], op=ALU.mult),
                          r=[(SK(pslot), 0), (SK(pslot), 1), "Eb"], w=[("smb", ptslot, 0), ("smb", ptslot, 1)])

                def stage_b(n):
                    p, d, r, bq = its[n]
                    s0 = bq * 128 * d + r
                    tok = slice(s0, s0 + 127 * d + 1, d)
                    nblk = 2 if bq > 0 else 1
                    q4 = n % 4
                    pq4 = (n - 1) % 4
                    ptslot = PTS[n % 3]
                    pT = smallb[:, ptslot, :]
                    bob = 6 + n % 2
                    for hh in range(2):
                        o_ = psb[bob][:, hh * 128:(hh + 1) * 128]
                        vl = [vtile(q4)[:, hh * 128:(hh + 1) * 128]]
                        rk = [("smbv", q4), ("smb", ptslot, hh)]
                        if bq > 0:
                            vl.append(vtile(pq4)[:, hh * 128:(hh + 1) * 128])
                            rk.append(("smbv", pq4))
                        for kb in range(nblk):
                            P.pe(_I("matmul", o_, vl[kb], pT[:, hh * 256 + kb * 128: hh * 256 + (kb + 1) * 128],
                                    start=(kb == 0), stop=(kb == nblk - 1)), r=rk, w=[PK(bob)])
                    ov = psb[bob][:, 0:256].rearrange("p (a b) -> p a b", a=2)
                    av = acc[:, :, tok]
                    if p == 0:
                        P.act(_I("activation", out=av, in_=ov, func=AF.Identity), r=[PK(bob)], w=ACCK)
                    else:
                        P.dve(_I("tensor_tensor", out=av, in0=ov, in1=av, op=ALU.add), r=[PK(bob)] + ACCK, w=ACCK)

                N_ = len(its)
                for n in range(min(DEPTH_A, N_)):
                    stage_a(n)
                for n in range(N_):
                    if n + DEPTH_A < N_:
                        stage_a(n + DEPTH_A)
                    stage_b(n)
                tmp = S(8)
                for tt in range(NTT):
                    ts = slice(tt * TT, (tt + 1) * TT)
                    P.dve(_I("reciprocal", out=tmp[0:64, :], in_=acc[64:128, 0, ts]), r=ACCK, w=[SK(8)])
                    P.dve(_I("tensor_tensor", out=big[0:64, c, ts], in0=acc[0:64, 0, ts], in1=tmp[0:64, :], op=ALU.mult),
                          r=ACCK + [SK(8)], w=[("big", c, tt)])
                    P.dve(_I("reciprocal", out=tmp[0:64, :], in_=acc[64:128, 1, ts]), r=ACCK + [SK(8)], w=[SK(8)])
                    P.dve(_I("tensor_tensor", out=big[64:128, c, ts], in0=acc[0:64, 1, ts], in1=tmp[0:64, :], op=ALU.mult),
                          r=ACCK + [SK(8)], w=[("big", c, tt)])
            P.dve(_I("memset", dummy[:, 1:2], 0.0), r=SUBK, w=WHOLEK + ["dummy"])

        def stage_mem(l, s):
            memf = scr[:, 0:4, :].rearrange("p a b -> p (a b)").rearrange("p (c m) -> p c m", c=8)
            MK = [SK(i) for i in range(4)]
            MNK = [("smb", i) for i in range(4)]; KNK = [SK(4), SK(5)]; VMK = [SK(6), SK(7)]
            P.dma(_I("dma_start", out=memf, in_=memT[s].rearrange("(c p) m -> p c m", p=128)), "memf", w=MK)
            b = bank()
            for c in range(8):
                sq = sqb[:, c % 2, 0:MEM]
                P.act(_I("activation", out=sq, in_=memf[:, c, :], func=AF.Square), r=MK, w=[("sqb", c % 2)])
                P.pe(_I("matmul", psb[b][:, 0:MEM], ones[:], sq, start=(c == 0), stop=(c == 7)),
                     r=[("sqb", c % 2), "ones"], w=[PK(b)])
            rs = rstd_from_ps(b, MEM, 1.0 / D, 0)
            for c in range(8):
                P.dve(_I("scalar_tensor_tensor", out=memn[:, c, :], in0=memf[:, c, :], scalar=spc(l, "g_memkv", c),
                                                            in1=rs, op0=ALU.mult, op1=ALU.mult),
                      r=MK + [("rsb", 0), "spt"], w=MNK)
            for i in range(4):
                wp, wk = w_next("mk%d" % i)
                bj = []
                for j in range(2):
                    b = bank(); bj.append(b)
                    mm_group(b, [wp[:, kc, j * 128:(j + 1) * 128] for kc in range(8)], [memn[:, kc, :] for kc in range(8)],
                             [wk] + MNK, N=MEM)
                b2 = bank()
                for j in range(2):
                    sq = sqb[:, j, 0:MEM]
                    P.act(_I("activation", out=sq, in_=psb[bj[j]][:, 0:MEM], func=AF.Square),
                          r=[PK(bj[j])], w=[("sqb", j)])
                    P.pe(_I("matmul", psb[b2][:, 0:MEM], ones[:], sq, start=(j == 0), stop=(j == 1)),
                         r=[("sqb", j), "ones"], w=[PK(b2)])
                rs = rstd_from_ps(b2, MEM, 1.0 / 256.0, 1)
                for j in range(2):
                    P.dve(_I("scalar_tensor_tensor",
                        out=kn[:, 2 * i + j, :], in0=psb[bj[j]][:, 0:MEM], scalar=spc(l, "mkg", j), in1=rs, op0=ALU.mult, op1=ALU.mult),
                        r=[PK(bj[j]), ("rsb", 1), "spt"], w=KNK)
            for i in range(4):
                wp, wk = w_next("mv%d" % i)
                for mb in range(2):
                    b = bank()
                    mm_group(b, [memn[:, kc, mb * 128:(mb + 1) * 128] for kc in range(8)], [wp[:, kc, :] for kc in range(8)],
                             [wk] + MNK, N=256)
                    P.act(_I("activation", out=vmem[:, mb, i * 256:(i + 1) * 256], in_=psb[b][:, 0:256],
                                                                  func=AF.Identity), r=[PK(b)], w=VMK)
            for i in range(4):
                wp, wk = w_next("mq%d" % i)
                for tt in range(NTT):
                    ts = slice(tt * TT, (tt + 1) * TT)
                    bj = []
                    for j in range(2):
                        b = bank(); bj.append(b)
                        mm_group(b, [wp[:, kc, j * 128:(j + 1) * 128] for kc in range(8)], [hn[:, kc, ts] for kc in range(8)],
                                 [wk] + hn_keys(tt))
                    b2 = bank()
                    for j in range(2):
                        sq = sqb[:, j, :]
                        P.act(_I("activation", out=sq, in_=psb[bj[j]][:, :], func=AF.Square),
                              r=[PK(bj[j])], w=[("sqb", j)])
                        P.pe(_I("matmul", psb[b2][:, :], ones[:], sq, start=(j == 0), stop=(j == 1)),
                             r=[("sqb", j), "ones"], w=[PK(b2)])
                    slot = tt % 2
                    rs = rstd_from_ps(b2, TT, 1.0 / 256.0, slot)
                    for j in range(2):
                        P.dve(_I("scalar_tensor_tensor",
                            out=big[:, 2 * i + j, ts], in0=psb[bj[j]][:, :], scalar=der[:, 4 + j:5 + j], in1=rs, op0=ALU.mult,
                            op1=ALU.mult), r=[PK(bj[j]), ("rsb", slot), "der"], w=[("big", 2 * i + j, tt)])
            for tt in range(NTT):
                ts = slice(tt * TT, (tt + 1) * TT)
                for i in range(4):
                    pts = []
                    for mb in range(2):
                        b = bank()
                        mm_group(b, [kn[:, 2 * i + j, mb * 128:(mb + 1) * 128] for j in range(2)],
                                 [big[:, 2 * i + j, ts] for j in range(2)], KNK + [("big", 2 * i + j, tt) for j in range(2)])
                        pT = smallb[:, mb + 2 * (i % 2), :]
                        pk = ("smb", mb + 2 * (i % 2))
                        P.act(_I("activation", out=pT, in_=psb[b][:, :], func=AF.Exp), r=[PK(b)], w=[pk])
                        pts.append((pT, pk))
                    bd_ = bank()
                    mm_group(bd_, [ones[:], ones[:]], [pts[0][0], pts[1][0]], ["ones", pts[0][1], pts[1][1]])
                    rd = S(8 + (i % 2))
                    P.act(_I("activation", out=rd, in_=psb[bd_][:, :], func=AF.Identity), r=[PK(bd_)],
                          w=[SK(8 + (i % 2))])
                    P.dve(_I("reciprocal", out=rd, in_=rd), r=[SK(8 + (i % 2))], w=[SK(8 + (i % 2))])
                    for j in range(2):
                        bn = bank()
                        mm_group(bn, [vmem[:, mb, i * 256 + j * 128: i * 256 + (j + 1) * 128] for mb in range(2)],
                                 [pts[0][0], pts[1][0]], VMK + [pts[0][1], pts[1][1]])
                        P.dve(_I("tensor_tensor", out=hn[:, 2 * i + j, ts], in0=psb[bn][:, :],
                                                                                      in1=rd, op=ALU.mult),
                              r=[PK(bn), SK(8 + (i % 2))], w=[("hn", 2 * i + j, tt)])
            for i in range(4):
                wp, wk = w_next("mo%d" % i)
                for tt in range(NTT):
                    ts = slice(tt * TT, (tt + 1) * TT)
                    for m in range(2):
                        b = bank()
                        mm_group(b, [wp[:, kc, m * 128:(m + 1) * 128] for kc in range(8)], [hn[:, kc, ts] for kc in range(8)],
                                 [wk] + hn_keys(tt))
                        resid_add(b, 2 * i + m, tt)

        def stage_ffn(l):
            for (a, bnd) in FFN_PARTS:
                for j in range(a, bnd):
                    jj = j - a
                    wp, wk = w_next("up%d" % j)
                    gbuf = scr[:, 0:2, :].rearrange("p a b -> p (a b)")
                    P.dve(_I("memset", gbuf[:, 0:2], 0.0), w=[SK(0)])
                    for tt in range(NTT):
                        ts = slice(tt * TT, (tt + 1) * TT)
                        bg = bank(); bu = bank()
                        mm_group(bg, [wp[:, kc, 0:128] for kc in range(8)], [hn[:, kc, ts] for kc in range(8)], [wk] + hn_keys(tt))
                        mm_group(bu, [wp[:, kc, 128:256] for kc in range(8)], [hn[:, kc, ts] for kc in range(8)], [wk] + hn_keys(tt))
                        P.act(_I("activation", out=gbuf[:, 2:2 + TT], in_=psb[bg][:, :], func=AF.Identity),
                              r=[PK(bg)], w=[SK(0), SK(1)])
                        n_it = st.setdefault("fi", 0); st["fi"] = n_it + 1
                        tslot = 2 + (n_it % 2)
                        t1 = S(tslot)
                        P.dve(_I("tensor_scalar", out=t1, in0=gbuf[:, 2:2 + TT], scalar1=spc(l, "fcw", 0 * 22 + j),
                                                               scalar2=spc(l, "fcb", j), op0=ALU.mult, op1=ALU.add),
                              r=[SK(0), SK(1), "spt"], w=[SK(tslot)])
                        for k in range(1, 3):
                            P.dve(_I("scalar_tensor_tensor", out=t1, in0=gbuf[:, 2 - k:2 - k + TT],
                                                                               scalar=spc(l, "fcw", k * 22 + j), in1=t1,
                                                                               op0=ALU.mult, op1=ALU.add),
                                  r=[SK(0), SK(1), SK(tslot), "spt"], w=[SK(tslot)])
                        P.dve(_I("tensor_copy", out=gbuf[:, 0:2], in_=gbuf[:, TT:TT + 2]), r=[SK(0), SK(1)], w=[SK(0)])
                        P.act(_I("activation", out=t1, in_=t1, func=AF.Gelu_apprx_tanh), r=[SK(tslot)], w=[SK(tslot)])
                        P.dve(_I("tensor_tensor", out=big[:, jj, ts], in0=psb[bu][:, :], in1=t1,
                                                                                    op=ALU.mult),
                              r=[PK(bu), SK(tslot)], w=[("big", jj, tt)])
                K = bnd - a
                for i in range(4):
                    wp, wk = w_next("dn%d_%d" % (a, i))
                    for tt in range(NTT):
                        ts = slice(tt * TT, (tt + 1) * TT)
                        for m in range(2):
                            b = bank()
                            mm_group(b, [wp[:, k, m * 128:(m + 1) * 128] for k in range(K)], [big[:, k, ts] for k in range(K)],
                                     [wk] + [("big", k, tt) for k in range(K)])
                            resid_add(b, 2 * i + m, tt)

        w_issue_upto(LOOKAHEAD)
        stages = ["norm1", "lru", "pool", "woA", "qkv", "attn", "woB", "norm2", "mem", "norm3", "ffn"]
        n_stage = len(stages) if debug_stage is None else stages.index(debug_stage) + 1
        for s in range(nseq):
            for c in range(8):
                P.dma(_I("dma_start", out=h[:, c, :], in_=xT[s, c * 128:(c + 1) * 128, :]), "hx",
                      w=[("h", c, tt) for tt in range(NTT)])
            for l in range(nlayers):
                par = 0
                layer_setup(l, par)
                fns = [lambda: rmsnorm_h(l, "g_mix"), lambda: stage_lru(l, par), lambda: stage_pool(l, par),
                       lambda: stage_wout("woA", [0, 1, 2, 3, 4]), lambda: stage_qkv(l), lambda: stage_attn(l),
                       lambda: stage_wout("woB", [0, 1, 2]), lambda: rmsnorm_h(l, "g_mem"), lambda: stage_mem(l, s),
                       lambda: rmsnorm_h(l, "g_ffn"), lambda: stage_ffn(l)]
                for f in fns[:n_stage]:
                    f()
            for c in range(8):
                P.dma(_I("dma_start", out=outT[s, c * 128:(c + 1) * 128, :], in_=h[:, c, :]), "ho",
                      r=[("h", c, tt) for tt in range(NTT)], w=["outT"])
        if debug_stage is not None:
            dbg_big = nc.dram_tensor("dbg_big", [128, 9 * SEQ], F32, kind="ExternalOutput").ap()
            dbg_hn = nc.dram_tensor("dbg_hn", [128, 8 * SEQ], F32, kind="ExternalOutput").ap()
            P.dma(_I("dma_start", out=dbg_big, in_=big[:].rearrange("p a b -> p (a b)")), "dbg",
                  r=[("big", c, tt) for c in range(9) for tt in range(NTT)], w=["dbgk"], q="pool")
            P.dma(_I("dma_start", out=dbg_hn, in_=hn[:].rearrange("p a b -> p (a b)")), "dbg",
                  r=[("hn", c, tt) for c in range(8) for tt in range(NTT)], w=["dbgk"], q="pool")
        P.op("sp", lambda e: None, reads=["outT", "dbgk"])
        P.emit()
    return nc


_CACHE = {}


def host_prepare(inp, nlayers=DEPTH):
    pieces = [weight_pieces_layer(inp, l) for l in range(DEPTH)]
    meta = [(nm, k, n) for nm, k, n, _ in pieces[0]]
    wst = np.concatenate([a for l in range(DEPTH) for _, _, _, a in pieces[l]], axis=1)
    wst = np.ascontiguousarray(wst, dtype=np.float32)
    E, cst = const_tables()
    shared = {"wst": wst, "spd": small_params(inp), "bdd": blockdiag_params(inp), "Ed": E, "cstd": cst}
    return meta, shared


def kernel(**inputs):
    inp = {k: np.asarray(v) for k, v in inputs.items()}
    x = inp["x"]; mem = inp["mem"]
    B = x.shape[0]
    nseq = B // NCORE
    meta, shared = host_prepare(inp)
    key = (nseq, DEPTH)
    nc = build_program(nseq=nseq, nlayers=DEPTH, piece_meta=meta)
    xT = np.ascontiguousarray(x.transpose(0, 2, 1))
    memT = np.ascontiguousarray(mem.transpose(0, 2, 1))
    in_maps = []
    for c in range(NCORE):
        m = dict(shared)
        m["xT"] = xT[c * nseq:(c + 1) * nseq]
        m["memT"] = memT[c * nseq:(c + 1) * nseq]
        in_maps.append(m)
    res = run_bass_kernel_spmd(nc, in_maps, core_ids=list(range(NCORE)))
    outT = np.concatenate([r["outT"] for r in res.results], axis=0)
    return np.ascontiguousarray(outT.transpose(0, 2, 1)).astype(np.float32)
```
